# Optimizing a Trainium2 kernel written in Bass

```python
import math
import jax, jax.numpy as jnp
from jax import lax
import numpy as np

D_MODEL = 1024
BATCH = 8
SEQ = 4096
DEPTH = 2

CTX_LEN = 256
GRID_W = 64
N_MIXERS = 2
EXPAND = 2
D_INNER = EXPAND * D_MODEL
HG_DK = 128
HG_HEADS = D_INNER // HG_DK
HG_DV = D_INNER // HG_HEADS
HG_CHUNK = 32
HG_IN_COLS = 5 * D_INNER
DA_HEADS = 16
DA_DQK = 64
DA_DV = 2 * DA_DQK
DA_IN_COLS = 4 * D_INNER
Q_BLOCK = 128
ROPE_THETA = 10000.0
EPS = 1e-6
N_HGRN_LAYERS = (DEPTH + N_MIXERS - 1) // N_MIXERS
N_DIFF_LAYERS = DEPTH // N_MIXERS

kernel_name = "hybrid_hgrn2_diffattn_dit_block"

F32 = jnp.float32


def rmsnorm(x, g):
    xf = x.astype(F32)
    y = xf * lax.rsqrt(jnp.mean(xf * xf, axis=-1, keepdims=True) + EPS)
    return (y * g.astype(F32)).astype(x.dtype)


def modulate(x, g, shift, scale):
    return rmsnorm(x, g) * (1 + scale) + shift


def _heads(t, n_heads):
    B, L, _ = t.shape
    return t.reshape(B, L, n_heads, -1).transpose(0, 2, 1, 3)


def gla_scan(q, k, v, logf, s0):
    B, H, L, DK = q.shape
    DV = v.shape[-1]
    n = L // HG_CHUNK

    def to_chunks(t):
        return jnp.moveaxis(t.reshape(B, H, n, HG_CHUNK, t.shape[-1]), 2, 0)

    mask = jnp.tril(jnp.ones((HG_CHUNK, HG_CHUNK), dtype=bool))

    def step(s, inp):
        qc, kc, vc, gc = inp
        b = jnp.cumsum(gc, axis=2)
        o_inter = jnp.einsum('bhck,bhkv->bhcv', qc * jnp.exp(b), s)
        d = b[:, :, :, None, :] - b[:, :, None, :, :]
        decay = jnp.exp(jnp.where(mask[:, :, None], d, -jnp.inf))
        a = jnp.einsum('bhik,bhjk,bhijk->bhij', qc, kc, decay)
        o_intra = jnp.einsum('bhij,bhjv->bhiv', a, vc)
        b_last = b[:, :, -1:, :]
        s_new = jnp.exp(b_last[:, :, 0, :])[..., None] * s + jnp.einsum(
            'bhck,bhcv->bhkv', kc * jnp.exp(b_last - b), vc)
        return s_new, o_inter + o_intra

    s_fin, o = lax.scan(step, s0, (to_chunks(q), to_chunks(k), to_chunks(v), to_chunks(logf)))
    o = jnp.moveaxis(o, 0, 2).reshape(B, H, L, DV)
    return o, s_fin


def hgrn2_project(h, w_in, lb_fwd, lb_bwd):
    q, zf, zb, i, gate = jnp.split(h @ w_in, 5, axis=-1)
    q = jax.nn.silu(q.astype(F32))

    def forget(z, lb):
        z = z.astype(F32)
        logf = jnp.log(lb + (1 - lb) * jax.nn.sigmoid(z))
        k = (1 - lb) * jax.nn.sigmoid(-z)
        return _heads(k, HG_HEADS), _heads(logf, HG_HEADS)

    kf, gf = forget(zf, lb_fwd)
    kb, gb = forget(zb, lb_bwd)
    return _heads(q, HG_HEADS), kf, gf, kb, gb, _heads(i.astype(F32), HG_HEADS), gate


def hgrn2_bidir(q, kf, gf, kb, gb, v, s0f, s0b):
    flip = lambda t: jnp.flip(t, axis=2)
    of, sf = gla_scan(q, kf, v, gf, s0f)
    ob, sb = gla_scan(flip(q), flip(kb), flip(v), flip(gb), s0b)
    return of + flip(ob), sf, sb


def hgrn2_readout(o, gate, norm_g, w_out, dtype):
    B, H, L, DV = o.shape
    o = o * lax.rsqrt(jnp.mean(o * o, axis=-1, keepdims=True) + EPS)
    o = o.transpose(0, 2, 1, 3).reshape(B, L, H * DV) * norm_g.astype(F32)
    return (o * jax.nn.silu(gate.astype(F32))).astype(dtype) @ w_out


def hgrn2_mixer(h_ctx, h_lat, w_in, lb_fwd, lb_bwd, norm_g, w_out, emit_ctx):
    dtype = h_lat.dtype
    B = h_lat.shape[0]
    s0 = jnp.zeros((B, HG_HEADS, HG_DK, HG_DV), F32)
    qc, kfc, gfc, kbc, gbc, vc, gc = hgrn2_project(h_ctx, w_in, lb_fwd, lb_bwd)
    o_ctx, s_cf, s_cb = hgrn2_bidir(qc, kfc, gfc, kbc, gbc, vc, s0, s0)
    ql, kfl, gfl, kbl, gbl, vl, gl = hgrn2_project(h_lat, w_in, lb_fwd, lb_bwd)
    o_lat, _, _ = hgrn2_bidir(ql, kfl, gfl, kbl, gbl, vl, s_cf, s_cb)
    y_lat = hgrn2_readout(o_lat, gl, norm_g, w_out, dtype)
    y_ctx = hgrn2_readout(o_ctx, gc, norm_g, w_out, dtype) if emit_ctx else None
    return y_ctx, y_lat


def axial_rope_tables(row, col, dtype):
    ax = DA_DQK // 2
    inv = 1.0 / (ROPE_THETA ** (jnp.arange(0, ax, 2, dtype=F32) / ax))
    L = row.shape[0]
    ang_r = row.astype(F32)[:, None] * inv
    ang_c = col.astype(F32)[:, None] * inv
    shp = (1, L, 1, 1, ax // 2)
    return (jnp.cos(ang_r).reshape(shp).astype(dtype), jnp.sin(ang_r).reshape(shp).astype(dtype),
            jnp.cos(ang_c).reshape(shp).astype(dtype), jnp.sin(ang_c).reshape(shp).astype(dtype))


def _rot(x, cos, sin):
    x1, x2 = jnp.split(x, 2, axis=-1)
    return jnp.concatenate([x1 * cos - x2 * sin, x2 * cos + x1 * sin], axis=-1)


def apply_axial_rope(x, cos_r, sin_r, cos_c, sin_c):
    xr, xc = jnp.split(x, 2, axis=-1)
    return jnp.concatenate([_rot(xr, cos_r, sin_r), _rot(xc, cos_c, sin_c)], axis=-1)


def diff_softmax(q, k, v, lam):
    s = jnp.einsum('bqhpd,bkhpd->bhpqk', q, k).astype(F32) * (DA_DQK ** -0.5)
    p = jax.nn.softmax(s, axis=-1)
    a = p[:, :, 0] - lam * p[:, :, 1]
    return jnp.einsum('bhqk,bkhv->bqhv', a, v.astype(F32))


def diff_attn_mixer(h_ctx, h_lat, w_in, lq1, lk1, lq2, lk2, subln_g, w_out, lambda_init, rope, emit_ctx):
    dtype = h_lat.dtype
    lam = (jnp.exp(jnp.sum(lq1.astype(F32) * lk1.astype(F32)))
           - jnp.exp(jnp.sum(lq2.astype(F32) * lk2.astype(F32))) + lambda_init)

    def project(h):
        B, L, _ = h.shape
        q, k, v, gate = jnp.split(h @ w_in, 4, axis=-1)
        return (q.reshape(B, L, DA_HEADS, 2, DA_DQK), k.reshape(B, L, DA_HEADS, 2, DA_DQK),
                v.reshape(B, L, DA_HEADS, DA_DV), gate)

    def readout(o, gate):
        B, L = o.shape[:2]
        o = o * lax.rsqrt(jnp.mean(o * o, axis=-1, keepdims=True) + EPS) * subln_g.astype(F32)
        o = (o * (1.0 - lambda_init)).reshape(B, L, D_INNER)
        return (o * jax.nn.silu(gate.astype(F32))).astype(dtype) @ w_out

    qc, kc, vc, gc = project(h_ctx)
    ql, kl, vl, gl = project(h_lat)
    ql = apply_axial_rope(ql, *rope)
    kl = apply_axial_rope(kl, *rope)
    keys = jnp.concatenate([kc, kl], axis=1)
    vals = jnp.concatenate([vc, vl], axis=1)

    B, L = ql.shape[:2]
    nb = L // Q_BLOCK
    qb = jnp.moveaxis(ql.reshape(B, nb, Q_BLOCK, DA_HEADS, 2, DA_DQK), 1, 0)
    o_lat = lax.map(lambda qq: diff_softmax(qq, keys, vals, lam), qb)
    o_lat = jnp.moveaxis(o_lat, 0, 1).reshape(B, L, DA_HEADS, DA_DV)
    y_lat = readout(o_lat, gl)
    y_ctx = readout(diff_softmax(qc, kc, vc, lam), gc) if emit_ctx else None
    return y_ctx, y_lat


def setup_inputs(seed: int = 0) -> dict:
    key = jax.random.key(seed)
    ks = jax.random.split(key, 20)

    def nrm(k, shape, s):
        return jax.random.normal(k, shape, F32) * s

    return {
        "x": nrm(ks[0], (BATCH, SEQ, D_MODEL), 1.0),
        "c": nrm(ks[1], (BATCH, D_MODEL), 1.0),
        "ctx": nrm(ks[2], (BATCH, CTX_LEN, D_MODEL), 1.0),
        "c_ctx": nrm(ks[3], (D_MODEL,), 1.0),
        "w_ada": nrm(ks[4], (DEPTH, D_MODEL, 3 * D_MODEL), 0.5 * D_MODEL ** -0.5),
        "b_ada": nrm(ks[5], (DEPTH, 3 * D_MODEL), 0.01),
        "norm_g": 1.0 + nrm(ks[6], (DEPTH, D_MODEL), 0.02),
        "hg_w_in": nrm(ks[7], (N_HGRN_LAYERS, D_MODEL, HG_IN_COLS), D_MODEL ** -0.5),
        "hg_lb_logits": nrm(ks[8], (N_HGRN_LAYERS + 1, 2, D_INNER), 0.5),
        "hg_norm_g": 1.0 + nrm(ks[9], (N_HGRN_LAYERS, D_INNER), 0.02),
        "hg_w_out": nrm(ks[10], (N_HGRN_LAYERS, D_INNER, D_MODEL), D_INNER ** -0.5),
        "da_w_in": nrm(ks[11], (N_DIFF_LAYERS, D_MODEL, DA_IN_COLS), D_MODEL ** -0.5),
        "da_lam_q1": nrm(ks[12], (N_DIFF_LAYERS, DA_DQK), 0.1),
        "da_lam_k1": nrm(ks[13], (N_DIFF_LAYERS, DA_DQK), 0.1),
        "da_lam_q2": nrm(ks[14], (N_DIFF_LAYERS, DA_DQK), 0.1),
        "da_lam_k2": nrm(ks[15], (N_DIFF_LAYERS, DA_DQK), 0.1),
        "da_subln_g": 1.0 + nrm(ks[16], (N_DIFF_LAYERS, DA_DV), 0.02),
        "da_w_out": nrm(ks[17], (N_DIFF_LAYERS, D_INNER, D_MODEL), D_INNER ** -0.5),
        "final_g": 1.0 + nrm(ks[18], (D_MODEL,), 0.02),
    }


def reference(x, c, ctx, c_ctx, w_ada, b_ada, norm_g, hg_w_in, hg_lb_logits, hg_norm_g, hg_w_out,
              da_w_in, da_lam_q1, da_lam_k1, da_lam_q2, da_lam_k2, da_subln_g, da_w_out, final_g):
    n_lat = x.shape[1]
    rows = n_lat // GRID_W
    row = jnp.broadcast_to(jnp.arange(rows, dtype=jnp.int32)[:, None], (rows, GRID_W)).reshape(-1)
    col = jnp.broadcast_to(jnp.arange(GRID_W, dtype=jnp.int32)[None, :], (rows, GRID_W)).reshape(-1)
    rope = axial_rope_tables(row, col, x.dtype)

    lb_all = jnp.cumsum(jax.nn.softmax(hg_lb_logits.astype(F32), axis=0), axis=0)

    sc = jax.nn.silu(c)
    scc = jax.nn.silu(c_ctx)
    x_lat, x_ctx = x, ctx
    for i in range(DEPTH):
        emit_ctx = i < DEPTH - 1
        shift, scale, gate = jnp.split(sc @ w_ada[i] + b_ada[i], 3, axis=-1)
        shift_c, scale_c, gate_c = jnp.split(scc @ w_ada[i] + b_ada[i], 3, axis=-1)
        h_lat = modulate(x_lat, norm_g[i], shift[:, None, :], scale[:, None, :])
        h_ctx = modulate(x_ctx, norm_g[i], shift_c, scale_c)
        j = i // N_MIXERS
        if i % N_MIXERS == 0:
            y_ctx, y_lat = hgrn2_mixer(h_ctx, h_lat, hg_w_in[j], lb_all[j, 0], lb_all[j, 1],
                                       hg_norm_g[j], hg_w_out[j], emit_ctx)
        else:
            lambda_init = 0.8 - 0.6 * math.exp(-0.3 * i)
            y_ctx, y_lat = diff_attn_mixer(h_ctx, h_lat, da_w_in[j], da_lam_q1[j], da_lam_k1[j],
                                           da_lam_q2[j], da_lam_k2[j], da_subln_g[j], da_w_out[j],
                                           lambda_init, rope, emit_ctx)
        x_lat = x_lat + gate[:, None, :] * y_lat
        if emit_ctx:
            x_ctx = x_ctx + gate_c * y_ctx
    return rmsnorm(x_lat, final_g)
```

```python
import contextlib
import math
import numpy as np
import concourse.bass as bass
import concourse.mybir as mybir
from concourse.bass_utils import run_bass_kernel_spmd

F32 = mybir.dt.float32
BF16 = mybir.dt.bfloat16
AF = mybir.ActivationFunctionType
ALU = mybir.AluOpType

D = 1024
L = 4096
CTX = 256
T = CTX + L
NT = T // 128
DI = 2048
NH = 16
EPS = 1e-6
SAME_ENG_SYNC = True
BLOCKS = [(0, 256)] + [(256 + 512 * i, 512) for i in range(8)]
LAMBDA_INIT1 = 0.8 - 0.6 * math.exp(-0.3 * 1)


class Dep:
    __slots__ = ("w", "r", "dsem", "name", "excl", "acc")

    def __init__(self, name="", excl=False, acc=False):
        self.w = {}
        self.r = {}
        self.dsem = None
        self.name = name
        self.excl = excl
        self.acc = acc


class KB:
    def __init__(self, nc, es):
        self.nc = nc
        self.es = es
        self.eng = {"pe": nc.tensor, "act": nc.scalar, "dve": nc.vector, "pool": nc.gpsimd, "sp": nc.sync}
        self.sem = {}
        self.cnt = {}
        self.semobj = {}
        for k in ("pe", "act", "dve", "pool"):
            s = es.enter_context(nc.semaphore("s_" + k))
            self.sem[k] = s
            self.cnt[k] = 0
            self.semobj["E" + k] = s
        self.dtotal = {}
        self.seen = {k: {} for k in self.eng}
        self.ndsem = 0
        self.ninst = {k: 0 for k in self.eng}

    def _collect(self, e, reads, writes):
        ev = {}
        for d in reads:
            for k, v in d.w.items():
                if ev.get(k, 0) < v:
                    ev[k] = v
        own = "E" + e
        for d in writes:
            if d.acc:
                continue
            for k, v in d.w.items():
                if k == own:
                    continue
                if ev.get(k, 0) < v:
                    ev[k] = v
            for k, v in d.r.items():
                if ev.get(k, 0) < v:
                    ev[k] = v
        if own in ev and (e == "pe" or not SAME_ENG_SYNC):
            del ev[own]
        return ev

    def _wait(self, e, ev):
        seen = self.seen[e]
        for k, v in ev.items():
            if k in self.dtotal:
                v = self.dtotal[k]
            if seen.get(k, 0) < v:
                self.eng[e].wait_ge(self.semobj[k], v)
                self.ninst[e] += 1
                seen[k] = v

    def _mark(self, key, val, reads, writes):
        for d in reads:
            if d.r.get(key, 0) < val:
                d.r[key] = val
        for d in writes:
            if d.acc:
                if d.w.get(key, 0) < val:
                    d.w[key] = val
                continue
            d.w = {key: val}
            d.r = {}

    def op(self, e, meth, reads, writes, inc=True, **kw):
        if any(d.excl for d in reads):
            writes = list(writes) + [d for d in reads if d.excl]
            reads = [d for d in reads if not d.excl]
        ev = self._collect(e, reads, writes)
        self._wait(e, ev)
        ins = getattr(self.eng[e], meth)(**kw)
        self.ninst[e] += 1
        if inc:
            self.cnt[e] += 1
            ins.then_inc(self.sem[e], 1)
            val = self.cnt[e]
        else:
            val = self.cnt[e] + 1
        self._mark("E" + e, val, reads, writes)
        return ins

    def dma(self, q, out, in_, sb, reads=(), writes=(), **kw):
        ev = self._collect(q, reads, writes)
        self._wait(q, ev)
        if sb.dsem is None:
            self.ndsem += 1
            name = "d%d" % self.ndsem
            s = self.es.enter_context(self.nc.semaphore(name))
            sb.dsem = name
            self.semobj[name] = s
            self.dtotal[name] = 0
        key = sb.dsem
        ins = self.eng[q].dma_start(out=out, in_=in_, **kw)
        self.ninst[q] += 1
        self.dtotal[key] += 16
        ins.then_inc(self.semobj[key], 16)
        self._mark(key, self.dtotal[key], reads, writes)
        return ins

    def barrier(self):
        ev = {"E" + k: v for k, v in self.cnt.items() if v > 0}
        for k, v in self.dtotal.items():
            if v > 0:
                ev[k] = v
        for e in self.eng:
            mine = dict(ev)
            mine.pop("E" + e, None)
            self._wait(e, mine)

    def final_wait(self, deps):
        ev = {}
        for d in deps:
            for k, v in d.w.items():
                ev[k] = max(ev.get(k, 0), v)
        self._wait("sp", ev)


class Rot:
    def __init__(self, bufs, excl=False):
        self.bufs = [(b, Dep(excl=excl)) for b in bufs]
        self.i = 0

    def next(self):
        b = self.bufs[self.i % len(self.bufs)]
        self.i += 1
        return b


def sb_rot(nc, es, name, n, shape, dt):
    return Rot([es.enter_context(nc.sbuf_tensor("%s%d" % (name, i), list(shape), dt)) for i in range(n)])


def build(dbg=None, heads0=NH, heads1=NH, skip0=False, eopt=""):
    nc = bass.Bass("TRN2", target_bir_lowering=False)

    def din(name, shape, dt=F32):
        return nc.dram_tensor(name, list(shape), dt, kind="ExternalInput").ap()

    xin = din("xin", [T, D])
    cfm = din("cfm", [128, 8, 2])
    w_ada = din("w_ada", [2, D, 3 * D])
    b_fm = din("b_fm", [2, 128, 24])
    b_row = din("b_row", [2, 1, 3 * D])
    g_fm = din("g_fm", [2, 128, 8])
    ident = din("ident", [128, 128])
    maskf = din("maskf", [128, 128])
    maskb = din("maskb", [128, 128])
    hg_w_in = din("hg_w_in", [D, 5 * DI])
    lbl = din("lbl", [128, 2, 2, NH])
    hg_ng = din("hg_ng", [128, NH])
    hg_w_out = din("hg_w_out", [DI, D])
    da_w_in = din("da_w_in", [D, 4 * DI])
    da_w_out = din("da_w_out", [DI, D])
    ropeC = din("ropeC", [128, L], BF16)
    ropeS = din("ropeS", [128, L], BF16)
    lamv = din("lamv", [128, 4, 64])
    subg = din("subg", [128, 1])
    fgr = din("fgr", [128, D])
    permm = din("permm", [128, 128])
    out = nc.dram_tensor("out", [L, D], F32, kind="ExternalOutput").ap()
    ofin1 = nc.dram_tensor("ofin1", [NH, 128, L], BF16, kind="ExternalOutput" if dbg == "E" else "Internal").ap()
    okind = "ExternalOutput" if dbg == "C" else "Internal"
    ofin0 = nc.dram_tensor("ofin0", [NH, 128, T], BF16, kind=okind).ap()
    x1 = nc.dram_tensor("x1", [T, D], F32, kind="ExternalOutput" if dbg == "D" else "Internal").ap()

    with contextlib.ExitStack() as es:
        kb = KB(nc, es)

        def sbuf(name, shape, dt, stack=es):
            return stack.enter_context(nc.sbuf_tensor(name, list(shape), dt))

        dumps = {}

        def dump(name, ap, deps, dt=F32):
            if dbg != "C":
                return
            if name in dumps:
                return
            import os
            sel = os.environ.get("DUMPS", "")
            if sel and not any(name.startswith(x) for x in sel.split(",")):
                return
            shp = list(ap.shape)
            dten = nc.dram_tensor("dump_" + name, shp, dt, kind="ExternalOutput").ap()
            dd_ = Dep()
            dumps[name] = dd_
            kb.dma("sp", dten, ap, dd_, reads=deps, writes=[dd_])

        def psum(name, shape, dt, stack=es):
            return stack.enter_context(nc.psum_tensor(name, list(shape), dt))

        ident_t = sbuf("ident_t", [128, 128], F32)
        ident_b = sbuf("ident_b", [128, 128], BF16)
        ones_t = sbuf("ones_t", [128, 128], F32)
        ones_b = sbuf("ones_b", [128, 128], BF16)
        mf_t = sbuf("mf_t", [128, 128], F32)
        mb_t = sbuf("mb_t", [128, 128], F32)
        d_const = Dep("const")
        kb.dma("sp", ident_t[:], ident[:, :], d_const, writes=[d_const])
        kb.dma("sp", mf_t[:], maskf[:, :], d_const, writes=[d_const])
        kb.dma("sp", mb_t[:], maskb[:, :], d_const, writes=[d_const])
        d_ones = Dep("ones")
        kb.op("pool", "memset", [], [d_ones], ap=ones_t[:], constant=1.0)
        kb.op("pool", "memset", [], [d_ones], ap=ones_b[:], constant=1.0)
        d_identb = Dep()
        kb.op("dve", "tensor_copy", [d_const], [d_identb], out=ident_b[:], in_=ident_t[:])

        aT = sbuf("aT", [128, 2, 8, 2], F32)
        shT = sbuf("shT", [128, 2, 8, 2], F32)
        gate_dram = nc.dram_tensor("gate_dram", [128, 3, D], F32, kind="Internal").ap()
        d_gdram = Dep("gdram")
        d_aT = Dep("aT"); d_shT = Dep("shT"); d_gate = [Dep("g0"), Dep("g1"), Dep("g2")]
        with contextlib.ExitStack() as es2:
            gate_bc = sbuf("gate_bc", [128, 3, D], F32, es2)
            sv = sbuf("sv", [128, 8, 2], F32, es2)
            sc = sbuf("sc", [128, 8, 2], F32, es2)
            th = sbuf("th", [128, 8, 2], F32, es2)
            wada = sbuf("wada", [128, 8, 3 * D], F32, es2)
            bfm = sbuf("bfm", [128, 2, 24], F32, es2)
            gfm = sbuf("gfm", [128, 2, 8], F32, es2)
            brow = sbuf("brow", [1, 3 * D], F32, es2)
            scbc = sbuf("scbc", [128, 8, 2, 128], F32, es2)
            modT = sbuf("modT", [128, 24, 2], F32, es2)
            psA = psum("psA", [128, 256, 2], F32, es2)
            psG = psum("psG", [128, 2, 512], F32, es2)
            d_sv = Dep(); d_sc = Dep(); d_th = Dep(); d_w = Dep(); d_bfm = Dep(); d_gfm = Dep(); d_brow = Dep()
            d_scbc = Dep(); d_modT = Dep(); d_psA = Dep(excl=True); d_psG = [Dep(excl=True), Dep(excl=True)]
            kb.dma("sp", sv[:], cfm[:, :, :], d_sv, writes=[d_sv])
            kb.dma("sp", bfm[:], b_fm.rearrange("l p j -> p l j"), d_bfm, writes=[d_bfm])
            kb.dma("sp", gfm[:], g_fm.rearrange("l p j -> p l j"), d_gfm, writes=[d_gfm])
            kb.op("act", "activation", [d_sv], [d_th], out=th[:], in_=sv[:], func=AF.Tanh, scale=0.5)
            kb.op("dve", "tensor_scalar", [d_th], [d_th], out=th[:], in0=th[:], scalar1=0.5, scalar2=0.5, op0=ALU.mult, op1=ALU.add)
            kb.op("dve", "tensor_mul", [d_th, d_sv], [d_sc], out=sc[:], in0=th[:], in1=sv[:])
            for kc in range(8):
                for s in range(2):
                    kb.op("dve", "tensor_scalar", [d_sc, d_ones], [d_scbc], out=scbc[:, kc, s, :], in0=ones_t[:],
                          scalar1=sc[:, kc, s:s + 1], scalar2=None, op0=ALU.mult)
            for l in range(2):
                kb.dma("sp", brow[:], b_row[l, :, :], d_brow, writes=[d_brow])
                for half in range(2):
                    kb.dma("sp", wada[:, half * 4:(half + 1) * 4, :],
                           w_ada[l, half * 512:(half + 1) * 512, :].rearrange("(kc p) n -> p kc n", p=128), d_w, writes=[d_w])
                for j in range(16):
                    for kc in range(8):
                        kb.op("pe", "matmul", [d_w, d_sc], [d_psA], inc=(j == 15 and kc == 7),
                              out=psA[:, j, :], lhsT=wada[:, kc, j * 128:(j + 1) * 128], rhs=sc[:, kc, :], start=(kc == 0), stop=(kc == 7))
                kb.op("dve", "tensor_tensor", [d_psA, d_bfm], [d_modT], out=modT[:, 0:16, :], in0=psA[:, 0:16, :],
                      in1=bfm[:, l, 0:16].unsqueeze(2).to_broadcast([128, 16, 2]), op=ALU.add)
                kb.op("dve", "tensor_copy", [d_modT], [d_shT], out=shT[:, l, :, :], in_=modT[:, 0:8, :])
                kb.op("dve", "scalar_tensor_tensor", [d_modT, d_gfm], [d_aT], out=aT[:, l, :, :], in0=modT[:, 8:16, :], scalar=1.0,
                      in1=gfm[:, l, :].unsqueeze(2).to_broadcast([128, 8, 2]), op0=ALU.add, op1=ALU.mult)
                for s in range(2):
                    if l == 1 and s == 1:
                        continue
                    gi = l * 2 + s
                    for nb in range(2):
                        c0 = 2048 + nb * 512
                        for kc in range(8):
                            kb.op("pe", "matmul", [d_scbc, d_w], [d_psG[nb]], inc=False,
                                  out=psG[:, nb, :], lhsT=scbc[:, kc, s, :], rhs=wada[:, kc, c0:c0 + 512], start=(kc == 0), stop=False)
                        kb.op("pe", "matmul", [d_ones, d_brow], [d_psG[nb]],
                              out=psG[:, nb, :], lhsT=ones_t[0:1, :], rhs=brow[0:1, c0:c0 + 512], start=False, stop=True)
                        kb.op("act", "activation", [d_psG[nb]], [d_gate[gi]],
                              out=gate_bc[:, gi, nb * 512:(nb + 1) * 512], in_=psG[:, nb, :], func=AF.Copy)
            kb.dma("sp", gate_dram[:, :, :], gate_bc[:], d_gate[0], reads=d_gate, writes=[d_gdram])

        kb.barrier()
        hT = sbuf("hT", [128, 8, T], BF16)
        d_h = [Dep("h%d" % i) for i in range(NT)]

        def prestage(l, es3, psT, d_psT, tiles, get_x):
            junk = sbuf("junk%d" % l, [128, D], F32, es3)
            d_junk = Dep()
            xn_r = sb_rot(nc, es3, "xn%d_" % l, 3, [128, D], F32)
            st_r = sb_rot(nc, es3, "st%d_" % l, 4, [128, 4], F32)
            def stage_a(tt):
                xt, d_x = get_x(tt)
                st, d_st = st_r.next()
                kb.op("act", "activation", [d_x], [d_junk, d_st], out=junk[:], in_=xt, func=AF.Square, accum_out=st[:, 0:1])
                kb.op("act", "activation", [d_st], [d_st], out=st[:, 1:2], in_=st[:, 0:1], func=AF.Ln, scale=1.0 / D, bias=EPS)
                kb.op("act", "activation", [d_st], [d_st], out=st[:, 2:3], in_=st[:, 1:2], func=AF.Exp, scale=-0.5)
                xn, d_xn = xn_r.next()
                kb.op("dve", "tensor_scalar", [d_x, d_st], [d_xn], out=xn[:], in0=xt, scalar1=st[:, 2:3], scalar2=None, op0=ALU.mult)
                return xn, d_xn

            def stage_b(tt, xn, d_xn):
                s = 1 if tt < 2 else 0
                for hb in range(2):
                    for c in range(4):
                        cc = hb * 4 + c
                        kb.op("pe", "transpose", [d_xn, d_const], [d_psT[hb]], inc=(c == 3),
                              out=psT[:, hb, c, :], in_=xn[:, cc * 128:(cc + 1) * 128], identity=ident_t[:])
                    for c in range(4):
                        cc = hb * 4 + c
                        kb.op("act", "activation", [d_psT[hb], d_aT, d_shT], [d_h[tt]],
                              out=hT[:, cc, tt * 128:(tt + 1) * 128], in_=psT[:, hb, c, :], func=AF.Identity,
                              scale=aT[:, l, cc, s:s + 1], bias=shT[:, l, cc, s:s + 1])
            tiles = list(tiles)
            cur_ = stage_a(tiles[0])
            for i_, tt in enumerate(tiles):
                nxt_ = stage_a(tiles[i_ + 1]) if i_ + 1 < len(tiles) else None
                stage_b(tt, *cur_)
                cur_ = nxt_

        with contextlib.ExitStack() as es3:
            psT = psum("psT0", [128, 2, 4, 128], F32, es3)
            d_psT = [Dep(excl=True), Dep(excl=True)]
            xt_r = sb_rot(nc, es3, "xt0_", 3, [128, D], F32)

            def get_x0(tt):
                xt, d_x = xt_r.next()
                kb.dma("sp", xt[:], xin[tt * 128:(tt + 1) * 128, :], d_x, writes=[d_x])
                return xt[:], d_x
            prestage(0, es3, psT, d_psT, range(NT), get_x0)
        kb.barrier()

        if dbg == "B":
            dbg_out = nc.dram_tensor("dbg", [128, 8, T], BF16, kind="ExternalOutput").ap()
            d_dbg = Dep()
            kb.dma("sp", dbg_out[:, :, :], hT[:], d_dbg, reads=d_h, writes=[d_dbg])
            kb.final_wait([d_dbg])
            return nc, kb

        d_of0 = Dep("ofin0", acc=True)
        with contextlib.ExitStack() as es4:
            lg = sbuf("lg", [128, 2, 2, NH], F32, es4)
            cA = sbuf("cA", [128, 2, NH], F32, es4)
            cB = sbuf("cB", [128, 2, NH], F32, es4)
            ncA = sbuf("ncA", [128, 2, NH], F32, es4)
            ngh = sbuf("ngh", [128, NH], F32, es4)
            tmpc = sbuf("tmpc", [128, 2, NH], F32, es4)
            d_lg = Dep(); d_cc = Dep(); d_ngh = Dep(); d_tmpc = Dep()
            kb.dma("sp", lg[:], lbl[:, :, :, :], d_lg, writes=[d_lg])
            kb.dma("sp", ngh[:], hg_ng[:, :], d_ngh, writes=[d_ngh])
            kb.op("dve", "tensor_scalar", [d_ngh], [d_ngh], out=ngh[:], in0=ngh[:], scalar1=0.5, scalar2=None, op0=ALU.mult)
            kb.op("dve", "tensor_tensor", [d_lg], [d_tmpc], out=tmpc[:], in0=lg[:, 1, :, :], in1=lg[:, 0, :, :], op=ALU.subtract)
            kb.op("act", "activation", [d_tmpc], [d_tmpc], out=tmpc[:], in_=tmpc[:], func=AF.Exp)
            kb.op("dve", "tensor_scalar", [d_tmpc], [d_tmpc], out=tmpc[:], in0=tmpc[:], scalar1=1.0, scalar2=None, op0=ALU.add)
            kb.op("dve", "reciprocal", [d_tmpc], [d_tmpc], out=tmpc[:], in_=tmpc[:])
            kb.op("dve", "tensor_scalar", [d_tmpc], [d_cc], out=cA[:], in0=tmpc[:], scalar1=-0.5, scalar2=0.5, op0=ALU.mult, op1=ALU.add)
            kb.op("dve", "tensor_scalar", [d_tmpc], [d_cc], out=cB[:], in0=tmpc[:], scalar1=0.5, scalar2=0.5, op0=ALU.mult, op1=ALU.add)
            kb.op("dve", "tensor_scalar", [d_tmpc], [d_cc], out=ncA[:], in0=tmpc[:], scalar1=0.5, scalar2=-0.5, op0=ALU.mult, op1=ALU.add)

            wst_r = sb_rot(nc, es4, "wst", 1, [128, 8, 128], F32)
            Wh_r = sb_rot(nc, es4, "Wh", 1, [128, 8, 5, 128], BF16)
            qs2 = sbuf("qs2", [128, T], BF16, es4)
            gs2 = sbuf("gs2", [128, T], BF16, es4)
            thz = [sbuf("thzf", [128, T], F32, es4), sbuf("thzb", [128, T], F32, es4)]
            v_all = sbuf("v_all", [128, NT, 128], BF16, es4)
            ofw = sbuf("ofw", [128, T], BF16, es4)
            NB = len(BLOCKS)
            d_qs2 = [Dep() for _ in range(NB)]; d_gs2 = [Dep() for _ in range(NB)]
            d_thz = [[Dep() for _ in range(NB)] for _ in range(2)]
            d_v = [Dep() for _ in range(NB)]; d_ofw = [Dep() for _ in range(NB)]
            f32_r = sb_rot(nc, es4, "tf", 13, [128, 512], F32)
            th_r = sb_rot(nc, es4, "tht", 2, [128, 512], F32)
            fin_r = sb_rot(nc, es4, "finr", 2, [128, 512], F32)
            b16_r = sb_rot(nc, es4, "tb", 10, [128, 512], BF16)
            sm_r = sb_rot(nc, es4, "sm", 4, [128, 8], F32)
            aTf_r = sb_rot(nc, es4, "aTf", 2, [128, 4, 128], BF16)
            aTb_r = sb_rot(nc, es4, "aTb", 2, [128, 4, 128], BF16)
            for r_ in (aTf_r, aTb_r):
                for (b_, d_) in r_.bufs:
                    kb.op("dve", "memset", [], [d_], ap=b_[:], constant=0.0)
            kdT_r = sb_rot(nc, es4, "kdT", 2, [128, 4, 128], BF16)
            Sb_r = sb_rot(nc, es4, "Sb", 2, [128, 128], BF16)
            S32 = sbuf("S32", [128, 128], F32, es4)
            d_S32 = Dep()
            of_r = sb_rot(nc, es4, "ofo", 2, [128, 512], BF16)
            psP = Rot([psum("psP%d" % i, [128, 512], F32, es4) for i in range(3)], excl=True)
            psO = Rot([psum("psO%d" % i, [128, 512], F32, es4) for i in range(2)], excl=True)
            psA_r = Rot([psum("psAa", [128, 512], F32, es4)], excl=True)
            psU_r = Rot([psum("psUu", [128, 512], F32, es4)], excl=True)
            psK_r = Rot([psum("psKk", [128, 1024], BF16, es4)], excl=True)

            def load_w(h):
                Wh, d_W = Wh_r.next()
                for wi in range(5):
                    wst, d_wst = wst_r.next()
                    c0 = wi * DI + h * 128
                    kb.dma("sp", wst[:], hg_w_in[:, c0:c0 + 128].rearrange("(kc p) j -> p kc j", p=128), d_wst, writes=[d_wst])
                    kb.op("act", "activation", [d_wst], [d_W], out=Wh[:, :, wi, :], in_=wst[:], func=AF.Copy)
                return Wh, d_W

            def stage1_gen(h, Wh, d_W, blocks=None):
                for bi, (t0, n) in enumerate(BLOCKS):
                    if blocks is not None and bi not in blocks:
                        continue
                    hd = d_h[t0 // 128:(t0 + n) // 128]
                    for wi in (0, 4, 1, 2):
                        ps, d_ps = psP.next()
                        for kc in range(8):
                            kb.op("pe", "matmul", [d_W] + hd, [d_ps], inc=(kc == 7), out=ps[:, 0:n], lhsT=Wh[:, kc, wi, :],
                                  rhs=hT[:, kc, t0:t0 + n], start=(kc == 0), stop=(kc == 7))
                        if wi in (0, 4):
                            tht, d_tht = th_r.next()
                            dst, d_dst = (qs2, d_qs2[bi]) if wi == 0 else (gs2, d_gs2[bi])
                            kb.op("act", "activation", [d_ps], [d_tht], out=tht[:, 0:n], in_=ps[:, 0:n], func=AF.Tanh, scale=0.5)
                            kb.op("dve", "scalar_tensor_tensor", [d_tht, d_ps], [d_dst], out=dst[:, t0:t0 + n], in0=tht[:, 0:n], scalar=1.0,
                                  in1=ps[:, 0:n], op0=ALU.add, op1=ALU.mult)
                        else:
                            di = wi - 1
                            kb.op("act", "activation", [d_ps], [d_thz[di][bi]], out=thz[di][:, t0:t0 + n], in_=ps[:, 0:n], func=AF.Tanh, scale=0.5)
                        yield
                    ps, d_ps = psP.next()
                    ncx = n // 128
                    for ci in range(ncx):
                        for kc in range(8):
                            kb.op("pe", "matmul", [d_W] + hd, [d_ps], inc=(kc == 7 and ci == ncx - 1), out=ps[:, ci * 128:(ci + 1) * 128],
                                  lhsT=hT[:, kc, t0 + ci * 128:t0 + (ci + 1) * 128], rhs=Wh[:, kc, 3, :], start=(kc == 0), stop=(kc == 7))
                    kb.op("act", "activation", [d_ps], [d_v[bi]], out=v_all[:, t0 // 128:(t0 + n) // 128, :].rearrange("p a b -> p (a b)"),
                          in_=ps[:, 0:n], func=AF.Copy)
                    yield

            def prep_gen(h, di, bi, P):
                t0, n = BLOCKS[bi]
                ncx = n // 128
                ref = 63 if di == 0 else 64
                last = 127 if di == 0 else 0
                lf, d_lf = f32_r.next()
                kb.op("act", "activation", [d_thz[di][bi], d_cc], [d_lf], out=lf[:, 0:n], in_=thz[di][:, t0:t0 + n], func=AF.Ln,
                      scale=cA[:, di, h:h + 1], bias=cB[:, di, h:h + 1])
                kk, d_kk = f32_r.next()
                kb.op("act", "activation", [d_thz[di][bi], d_cc], [d_kk], out=kk[:, 0:n], in_=thz[di][:, t0:t0 + n], func=AF.Identity,
                      scale=ncA[:, di, h:h + 1], bias=cA[:, di, h:h + 1])
                yield
                pre, d_pre = f32_r.next()
                for ci in range(ncx):
                    kb.op("dve", "tensor_tensor_scan", [d_lf, d_ones], [d_pre], out=pre[:, ci * 128:(ci + 1) * 128], data0=ones_t[:, 0:128],
                          data1=lf[:, ci * 128:(ci + 1) * 128], initial=0.0, op0=ALU.mult, op1=ALU.add)
                    if ci % 2 == 1 and ci < ncx - 1:
                        yield
                if di == 0:
                    dd, d_dd = pre, d_pre
                else:
                    dd, d_dd = f32_r.next()
                    kb.op("dve", "tensor_tensor", [d_lf, d_pre], [d_dd], out=dd[:, 0:n], in0=lf[:, 0:n], in1=pre[:, 0:n], op=ALU.subtract)
                sm, d_sm = sm_r.next()
                kb.op("dve", "tensor_scalar", [d_dd], [d_sm], out=sm[:, 0:ncx], in0=dd[:, 0:n].rearrange("p (c t) -> p c t", t=128)[:, :, ref],
                      scalar1=-1.0, scalar2=None, op0=ALU.mult)
                yield
                E, d_E = f32_r.next(); Er, d_Er = f32_r.next(); Ek, d_Ek = f32_r.next(); Ed, d_Ed = f32_r.next()
                for ci in range(ncx):
                    sl = slice(ci * 128, (ci + 1) * 128)
                    if di == 0:
                        kb.op("act", "activation", [d_dd], [d_E], out=E[:, sl], in_=dd[:, sl], func=AF.Exp)
                    else:
                        kb.op("act", "activation", [d_dd, d_pre], [d_E], out=E[:, sl], in_=dd[:, sl], func=AF.Exp,
                              bias=pre[:, ci * 128 + 127:ci * 128 + 128])
                    kb.op("act", "activation", [d_dd, d_sm], [d_Er], out=Er[:, sl], in_=dd[:, sl], func=AF.Exp, bias=sm[:, ci:ci + 1])
                    kb.op("act", "activation", [d_dd], [d_Ek], out=Ek[:, sl], in_=dd[:, sl], func=AF.Exp, scale=-1.0,
                          bias=dd[:, ci * 128 + ref:ci * 128 + ref + 1])
                    kb.op("act", "activation", [d_dd], [d_Ed], out=Ed[:, sl], in_=dd[:, sl], func=AF.Exp, scale=-1.0,
                          bias=dd[:, ci * 128 + last:ci * 128 + last + 1])
                    yield
                qe, d_qe = b16_r.next(); qr, d_qr = b16_r.next(); kr, d_kr = b16_r.next(); kd, d_kd = b16_r.next()
                kb.op("dve", "tensor_tensor", [d_qs2[bi], d_E], [d_qe], out=qe[:, 0:n], in0=qs2[:, t0:t0 + n], in1=E[:, 0:n], op=ALU.mult)
                kb.op("dve", "tensor_tensor", [d_qs2[bi], d_Er], [d_qr], out=qr[:, 0:n], in0=qs2[:, t0:t0 + n], in1=Er[:, 0:n], op=ALU.mult)
                yield
                kb.op("dve", "tensor_tensor", [d_kk, d_Ek], [d_kr], out=kr[:, 0:n], in0=kk[:, 0:n], in1=Ek[:, 0:n], op=ALU.mult)
                kb.op("dve", "tensor_tensor", [d_kk, d_Ed], [d_kd], out=kd[:, 0:n], in0=kk[:, 0:n], in1=Ed[:, 0:n], op=ALU.mult)
                P.update(E=E, d_E=d_E, qe=qe, d_qe=d_qe, qr=qr, d_qr=d_qr, kr=kr, d_kr=d_kr, kd=kd, d_kd=d_kd)

            state = {}

            def chunks(h, di, bi, P, pgen=None, sgen=None):
                t0, n = BLOCKS[bi]
                ncx = n // 128
                last = 127 if di == 0 else 0
                mask_t = mf_t if di == 0 else mb_t
                aT_r = aTf_r if di == 0 else aTb_r
                E, d_E, qe, d_qe, qr, d_qr, kr, d_kr, kd, d_kd = (P[k] for k in ("E", "d_E", "qe", "d_qe", "qr", "d_qr", "kr", "d_kr", "kd", "d_kd"))

                def step(k=1):
                    for _ in range(k):
                        if pgen is not None:
                            next(pgen, None)
                        if sgen is not None:
                            next(sgen, None)
                pa, d_pa = psA_r.next()
                pk, d_pk = psK_r.next()
                for ci in range(ncx):
                    sl = slice(ci * 128, (ci + 1) * 128)
                    kb.op("pe", "matmul", [d_kr, d_qr], [d_pa], inc=(ci == ncx - 1), out=pa[:, sl], lhsT=kr[:, sl], rhs=qr[:, sl], start=True, stop=True,
                          skip_group_check=True)
                for ci in range(ncx):
                    sl = slice(ci * 128, (ci + 1) * 128)
                    kb.op("pe", "transpose", [d_kd, d_identb], [d_pk], inc=(ci == ncx - 1), out=pk[:, sl], in_=kd[:, sl], identity=ident_b[:])
                step(2 if sgen is not None else 1)
                aS, d_aS = aT_r.next()
                pa3 = pa[:, 0:n].rearrange("p (c t) -> p c t", t=128)
                if di == 0:
                    kb.op("dve", "tensor_tensor", [d_pa, d_const], [d_aS], out=aS[0:64, 0:ncx, :], in0=pa3[0:64, :, :],
                          in1=mask_t[0:64, :].unsqueeze(1).to_broadcast([64, ncx, 128]), op=ALU.mult)
                    kb.op("dve", "tensor_tensor", [d_pa, d_const], [d_aS], out=aS[64:128, 0:ncx, 64:128], in0=pa3[64:128, :, 64:128],
                          in1=mask_t[64:128, 64:128].unsqueeze(1).to_broadcast([64, ncx, 64]), op=ALU.mult)
                else:
                    kb.op("dve", "tensor_tensor", [d_pa, d_const], [d_aS], out=aS[0:64, 0:ncx, 0:64], in0=pa3[0:64, :, 0:64],
                          in1=mask_t[0:64, 0:64].unsqueeze(1).to_broadcast([64, ncx, 64]), op=ALU.mult)
                    kb.op("dve", "tensor_tensor", [d_pa, d_const], [d_aS], out=aS[64:128, 0:ncx, :], in0=pa3[64:128, :, :],
                          in1=mask_t[64:128, :].unsqueeze(1).to_broadcast([64, ncx, 128]), op=ALU.mult)
                kT_, d_kT = kdT_r.next()
                kb.op("act", "activation", [d_pk], [d_kT], out=kT_[:, 0:ncx, :].rearrange("p c t -> p (c t)"), in_=pk[:, 0:n], func=AF.Copy)
                step(2 if sgen is not None else 1)
                pu, d_pu = psU_r.next()
                po, d_po = psO.next()
                for ci in range(ncx):
                    sl = slice(ci * 128, (ci + 1) * 128)
                    tile = t0 // 128 + ci
                    kb.op("pe", "matmul", [d_kT, d_v[bi]], [d_pu], inc=(ci == ncx - 1), out=pu[:, sl], lhsT=kT_[:, ci, :], rhs=v_all[:, tile, :],
                          start=True, stop=True, skip_group_check=True)
                for ci in range(ncx):
                    sl = slice(ci * 128, (ci + 1) * 128)
                    tile = t0 // 128 + ci
                    kb.op("pe", "matmul", [d_v[bi], d_aS], [d_po], inc=False, out=po[:, sl], lhsT=v_all[:, tile, :], rhs=aS[:, ci, :],
                          start=(ci == 0), stop=False, skip_group_check=True)
                step(2 if sgen is not None else 1)
                order_c = list(range(ncx)) if di == 0 else list(range(ncx - 1, -1, -1))
                for ci in order_c:
                    sl = slice(ci * 128, (ci + 1) * 128)
                    Sb, d_Sb = state["Sb"]
                    kb.op("pe", "matmul", [d_Sb, d_qe], [d_po], out=po[:, sl], lhsT=Sb[:], rhs=qe[:, sl], start=False, stop=True, skip_group_check=True)
                    dl = ci * 128 + last
                    kb.op("dve", "scalar_tensor_tensor", [d_E, d_pu, d_S32], [d_S32], out=S32[:], in0=S32[:], scalar=E[:, dl:dl + 1], in1=pu[:, sl],
                          op0=ALU.mult, op1=ALU.add)
                    Sb, d_Sb = Sb_r.next()
                    kb.op("dve", "tensor_copy", [d_S32], [d_Sb], out=Sb[:], in_=S32[:])
                    state["Sb"] = (Sb, d_Sb)
                    step()
                return po, d_po

            def finish_gen(h, di, bi, po, d_po):
                t0, n = BLOCKS[bi]
                if di == 0:
                    kb.op("act", "activation", [d_po], [d_ofw[bi]], out=ofw[:, t0:t0 + n], in_=po[:, 0:n], func=AF.Copy)
                    return
                ot, d_ot = fin_r.next()
                kb.op("dve", "tensor_tensor", [d_po, d_ofw[bi]], [d_ot], out=ot[:, 0:n], in0=po[:, 0:n], in1=ofw[:, t0:t0 + n], op=ALU.add)
                yield
                osq, d_osq = b16_r.next()
                kb.op("act", "activation", [d_ot], [d_osq], out=osq[:, 0:n], in_=ot[:, 0:n], func=AF.Square)
                pss, d_pss = psP.next()
                kb.op("pe", "matmul", [d_ones, d_osq], [d_pss], out=pss[:, 0:n], lhsT=ones_b[:], rhs=osq[:, 0:n], start=True, stop=True)
                yield
                rs, d_rs = fin_r.next()
                kb.op("act", "activation", [d_pss], [d_rs], out=rs[:, 0:n], in_=pss[:, 0:n], func=AF.Ln, scale=1.0 / 128, bias=EPS * 4)
                kb.op("act", "activation", [d_rs], [d_rs], out=rs[:, 0:n], in_=rs[:, 0:n], func=AF.Exp, scale=-0.5)
                yield
                kb.op("dve", "tensor_tensor", [d_rs, d_ot], [d_ot], out=ot[:, 0:n], in0=ot[:, 0:n], in1=rs[:, 0:n], op=ALU.mult)
                of_, d_of = of_r.next()
                kb.op("dve", "scalar_tensor_tensor", [d_gs2[bi], d_ngh, d_ot], [d_of], out=of_[:, 0:n], in0=gs2[:, t0:t0 + n],
                      scalar=ngh[:, h:h + 1], in1=ot[:, 0:n], op0=ALU.mult, op1=ALU.mult)
                kb.dma("pool", ofin0[h, :, t0:t0 + n], of_[:, 0:n], d_of, reads=[d_of], writes=[d_of0])

            def seq_gen(*gens):
                for g_ in gens:
                    if g_ is not None:
                        for _ in g_:
                            yield

            Wcur = load_w(0) if heads0 > 0 else None
            if heads0 > 0:
                for _ in stage1_gen(0, *Wcur):
                    pass
            for h in range(heads0):
                Wn = load_w(h + 1) if h + 1 < heads0 else None
                for di in range(2):
                    kb.op("dve", "memset", [], [d_S32], ap=S32[:], constant=0.0)
                    Sb, d_Sb = Sb_r.next()
                    kb.op("dve", "memset", [], [d_Sb], ap=Sb[:], constant=0.0)
                    state["Sb"] = (Sb, d_Sb)
                    order = list(range(NB)) if di == 0 else [0] + list(range(NB - 1, 0, -1))
                    P = {}
                    for _ in prep_gen(h, di, order[0], P):
                        pass
                    sgen = None
                    for i, bi in enumerate(order):
                        Pn = {}
                        pgen = prep_gen(h, di, order[i + 1], Pn) if i + 1 < len(order) else None
                        po, d_po = chunks(h, di, bi, P, pgen, sgen)
                        if pgen is not None:
                            for _ in pgen:
                                pass
                        if sgen is not None:
                            for _ in sgen:
                                pass
                        s1 = stage1_gen(h + 1, *Wn, blocks=(bi,)) if (di == 1 and Wn is not None) else None
                        sgen = seq_gen(finish_gen(h, di, bi, po, d_po), s1)
                        if di == 0:
                            for _ in sgen:
                                pass
                            sgen = None
                        P = Pn
                    if sgen is not None:
                        for _ in sgen:
                            pass

        if dbg == "C":
            kb.final_wait([d_of0] + list(dumps.values()))
            return nc, kb
        kb.barrier()

        d_x1 = Dep("x1", acc=True)
        with contextlib.ExitStack() as es5:
            Wo = sbuf("Wo0", [128, 16, D], BF16, es5)
            d_Wo = Dep()
            wos_r = sb_rot(nc, es5, "wos0_", 2, [128, 2, D], F32)
            for k2 in range(8):
                ws, d_ws = wos_r.next()
                kb.dma("sp", ws[:], hg_w_out[k2 * 256:(k2 + 1) * 256, :].rearrange("(k p) n -> p k n", p=128), d_ws, writes=[d_ws])
                kb.op("act", "activation", [d_ws], [d_Wo], out=Wo[:, k2 * 2:(k2 + 1) * 2, :], in_=ws[:], func=AF.Copy)
            gbc = sbuf("gbc0", [128, 2, D], F32, es5)
            d_gbc = Dep()
            kb.dma("sp", gbc[:], gate_dram[:, 0:2, :], d_gbc, reads=[d_gdram], writes=[d_gbc])
            ofb_r = sb_rot(nc, es5, "ofb0_", 2, [128, 16, 512], BF16)
            xt_r = sb_rot(nc, es5, "xt1_", 3, [128, D], F32)
            xnw_r = sb_rot(nc, es5, "xnw_", 3, [128, D], F32)
            psY = Rot([psum("psY%d" % i, [128, 512], F32, es5) for i in range(4)], excl=True)
            psT = psum("psT1", [128, 2, 4, 128], F32, es5)
            d_psT = [Dep(excl=True), Dep(excl=True)]
            cur = {}

            loaded = {}

            def load_ofb(bi):
                if bi in loaded or bi >= len(BLOCKS):
                    return
                t0, n = BLOCKS[bi]
                ofb, d_ofb = ofb_r.next()
                kb.dma("sp", ofb[:, :, 0:n], ofin0[:, :, t0:t0 + n].rearrange("h p t -> p h t"), d_ofb, reads=[d_of0], writes=[d_ofb])
                loaded[bi] = (ofb, d_ofb, t0)

            def get_x1(tt):
                for bi, (t0, n) in enumerate(BLOCKS):
                    if t0 <= tt * 128 < t0 + n:
                        break
                if cur.get("bi") != bi:
                    load_ofb(bi)
                    ofb, d_ofb, t0_ = loaded[bi]
                    cur.update(bi=bi, ofb=ofb, d_ofb=d_ofb, t0=t0_)
                    load_ofb(bi + 1)
                ofb, d_ofb = cur["ofb"], cur["d_ofb"]
                off = tt * 128 - cur["t0"]
                xt, d_x = xt_r.next()
                kb.dma("sp", xt[:], xin[tt * 128:(tt + 1) * 128, :], d_x, writes=[d_x])
                xnw, d_xnw = xnw_r.next()
                s_ = 1 if tt < 2 else 0
                for half in range(2):
                    py, d_py = psY.next()
                    hs = slice(half * 512, (half + 1) * 512)
                    for kc in range(16):
                        kb.op("pe", "matmul", [d_ofb, d_Wo], [d_py], inc=(kc == 15), out=py[:, :], lhsT=ofb[:, kc, off:off + 128],
                              rhs=Wo[:, kc, hs], start=(kc == 0), stop=(kc == 15))
                    kb.op("dve", "tensor_tensor", [d_py, d_gbc], [d_xnw], out=xnw[:, hs], in0=py[:, :], in1=gbc[:, s_, hs], op=ALU.mult)
                    kb.op("dve", "tensor_tensor", [d_xnw, d_x], [d_xnw], out=xnw[:, hs], in0=xnw[:, hs], in1=xt[:, hs], op=ALU.add)
                if tt >= 2:
                    kb.dma("pool", x1[tt * 128:(tt + 1) * 128, :], xnw[:], d_xnw, reads=[d_xnw], writes=[d_x1])
                return xnw[:], d_xnw
            prestage(1, es5, psT, d_psT, range(NT), get_x1)
        kb.barrier()

        if dbg == "D":
            dbg_out = nc.dram_tensor("dbg", [128, 8, T], BF16, kind="ExternalOutput").ap()
            d_dbg = Dep()
            kb.dma("sp", dbg_out[:, :, :], hT[:], d_dbg, reads=d_h, writes=[d_dbg])
            kb.final_wait([d_dbg, d_x1])
            return nc, kb

        d_of1 = Dep("ofin1", acc=True)
        with contextlib.ExitStack() as es6:
            ropeC_t = sbuf("ropeC_t", [128, L], BF16, es6)
            ropeS_t = sbuf("ropeS_t", [128, L], BF16, es6)
            d_rope = Dep()
            kb.dma("sp", ropeC_t[:], ropeC[:, :], d_rope, writes=[d_rope])
            kb.dma("sp", ropeS_t[:], ropeS[:, :], d_rope, writes=[d_rope])
            lv = sbuf("lv", [128, 4, 64], F32, es6)
            lp = sbuf("lp", [128, 2, 64], F32, es6)
            ls = sbuf("ls", [128, 4], F32, es6)
            sg = sbuf("sg", [128, 2], F32, es6)
            d_lv = Dep(); d_lp = Dep(); d_ls = Dep(); d_sg = Dep()
            kb.dma("sp", lv[:], lamv[:, :, :], d_lv, writes=[d_lv])
            kb.dma("sp", sg[:, 0:1], subg[:, :], d_sg, writes=[d_sg])
            kb.op("dve", "tensor_tensor", [d_lv], [d_lp], out=lp[:, 0, :], in0=lv[:, 0, :], in1=lv[:, 1, :], op=ALU.mult)
            kb.op("dve", "tensor_tensor", [d_lv, d_lp], [d_lp], out=lp[:, 1, :], in0=lv[:, 2, :], in1=lv[:, 3, :], op=ALU.mult)
            kb.op("dve", "tensor_reduce", [d_lp], [d_ls], out=ls[:, 0:2], in_=lp[:], axis=mybir.AxisListType.X, op=ALU.add)
            kb.op("act", "activation", [d_ls], [d_ls], out=ls[:, 0:2], in_=ls[:, 0:2], func=AF.Exp)
            kb.op("dve", "tensor_tensor", [d_ls], [d_ls], out=ls[:, 2:3], in0=ls[:, 1:2], in1=ls[:, 0:1], op=ALU.subtract)
            kb.op("dve", "tensor_scalar", [d_ls], [d_ls], out=ls[:, 3:4], in0=ls[:, 2:3], scalar1=-LAMBDA_INIT1, scalar2=None, op0=ALU.add)
            kb.op("dve", "tensor_scalar", [d_sg], [d_sg], out=sg[:, 1:2], in0=sg[:, 0:1], scalar1=(1.0 - LAMBDA_INIT1), scalar2=None, op0=ALU.mult)

            wst_r = sb_rot(nc, es6, "wsa", 1, [128, 8, 128], F32)
            NSLOT = 2
            Wb_s = [sbuf("Wba%d" % i, [128, 8, 4, 128], BF16, es6) for i in range(NSLOT)]
            qb_r = sb_rot(nc, es6, "qb16_", 2, [128, 512], BF16)
            perm_f = sbuf("perm_f", [128, 128], F32, es6)
            perm_b = sbuf("perm_b", [128, 128], BF16, es6)
            d_perm = Dep()
            kb.dma("sp", perm_f[:], permm[:, :], d_perm, writes=[d_perm])
            kb.op("dve", "tensor_copy", [d_perm], [d_perm], out=perm_b[:], in_=perm_f[:])
            qT_s = [sbuf("qTa%d" % i, [128, L], BF16, es6) for i in range(NSLOT)]
            kT_s = [sbuf("kTa%d" % i, [128, T], BF16, es6) for i in range(NSLOT)]
            Va_s = [sbuf("Vaa%d" % i, [128, NT, 130], BF16, es6) for i in range(NSLOT)]
            gs_s = [sbuf("gsa%d" % i, [128, L], BF16, es6) for i in range(NSLOT)]
            d_Wb = [Dep() for _ in range(NSLOT)]
            d_qT = [[Dep() for _ in range(8)] for _ in range(NSLOT)]
            d_kT = [[Dep() for _ in range(9)] for _ in range(NSLOT)]
            d_Va = [[Dep() for _ in range(9)] for _ in range(NSLOT)]
            d_gs = [[Dep() for _ in range(8)] for _ in range(NSLOT)]
            d_Vone = [Dep() for _ in range(NSLOT)]
            for i in range(NSLOT):
                kb.op("dve", "memset", [], [d_Vone[i]], ap=Va_s[i][:, :, 128:130], constant=1.0)
            tf_r = sb_rot(nc, es6, "tfa", 3, [128, 512], F32)
            PT_r = sb_rot(nc, es6, "PTa", 3, [128, 1024], BF16)
            accs_r = sb_rot(nc, es6, "accs", 1, [128, 8, 129], F32)
            sm_r = sb_rot(nc, es6, "sma", 4, [128, 16], F32)
            o_r = sb_rot(nc, es6, "oa", 4, [128, 128], F32)
            t_r = sb_rot(nc, es6, "ta", 2, [128, 128], F32)
            on_r = sb_rot(nc, es6, "ona", 4, [128, 128], F32)
            ofo_r = sb_rot(nc, es6, "ofoa", 2, [128, 512], BF16)
            psS = Rot([psum("psS%d" % i, [128, 1024], F32, es6) for i in range(2)], excl=True)
            accb = [psum("accb%d" % i, [128, 512], F32, es6) for i in range(3)]
            d_acc = Dep(excl=True)
            pm = psum("pmisc", [128, 512], F32, es6)
            d_pm = Dep(excl=True)

            pst = {"mid": False}

            def proj_gen(h, sl_):
                Wb, qT, kT, Va, gs = Wb_s[sl_], qT_s[sl_], kT_s[sl_], Va_s[sl_], gs_s[sl_]
                for wi in range(4):
                    wst, d_wst = wst_r.next()
                    c0 = wi * DI + h * 128
                    kb.dma("sp", wst[:], da_w_in[:, c0:c0 + 128].rearrange("(kc p) j -> p kc j", p=128), d_wst, writes=[d_wst])
                    kb.op("dve", "tensor_copy", [d_wst], [d_Wb[sl_]], out=Wb[:, :, wi, :], in_=wst[:])
                    yield
                dW = d_Wb[sl_]

                def half_group(slot, t0, c0, n):
                    hd = d_h[(t0 + c0) // 128:(t0 + c0 + n) // 128]
                    for kc in range(8):
                        kb.op("pe", "matmul", [dW] + hd, [d_pm], inc=(kc == 7), out=pm[:, c0:c0 + n], lhsT=Wb[:, kc, slot, :],
                              rhs=hT[:, kc, t0 + c0:t0 + c0 + n], start=(kc == 0), stop=(kc == 7))
                half_group(1, 0, 0, 256)
                kb.op("dve", "tensor_copy", [d_pm], [d_kT[sl_][0]], out=kT[:, 0:256], in_=pm[:, 0:256])
                yield
                for b in range(8):
                    t0 = 256 + 512 * b
                    l0 = 512 * b
                    for (slot, dst, ddst, doff) in ((0, qT, d_qT[sl_][b], l0), (1, kT, d_kT[sl_][b + 1], t0)):
                        half_group(slot, t0, 0, 256)
                        pst["mid"] = True
                        yield
                        half_group(slot, t0, 256, 256)
                        pst["mid"] = False
                        qb16, d_qb16 = qb_r.next()
                        kb.op("dve", "tensor_copy", [d_pm], [d_qb16], out=qb16[:], in_=pm[:, :])
                        t1, d_t1 = tf_r.next()
                        kb.op("dve", "tensor_tensor", [d_pm, d_rope], [d_t1], out=t1[:], in0=pm[:, :], in1=ropeC_t[:, l0:l0 + 512], op=ALU.mult)
                        yield
                        kb.op("pe", "matmul", [d_qb16, d_perm], [d_pm], out=pm[:, :], lhsT=perm_b[:], rhs=qb16[:], start=True, stop=True)
                        t2, d_t2 = tf_r.next()
                        kb.op("dve", "tensor_tensor", [d_pm, d_rope], [d_t2], out=t2[:], in0=pm[:, :], in1=ropeS_t[:, l0:l0 + 512], op=ALU.mult)
                        kb.op("dve", "tensor_tensor", [d_t1, d_t2], [ddst], out=dst[:, doff:doff + 512], in0=t1[:], in1=t2[:], op=ALU.add)
                        yield
                    half_group(3, t0, 0, 256)
                    pst["mid"] = True
                    yield
                    half_group(3, t0, 256, 256)
                    pst["mid"] = False
                    g_sb, d_gsb = tf_r.next()
                    kb.op("dve", "tensor_copy", [d_pm], [d_gsb], out=g_sb[:], in_=pm[:, :])
                    yield
                    yield
                    e_, d_e = tf_r.next()
                    kb.op("act", "activation", [d_gsb], [d_e], out=e_[:], in_=g_sb[:], func=AF.Exp, scale=-1.0)
                    kb.op("act", "activation", [d_e], [d_e], out=e_[:], in_=e_[:], func=AF.Ln, bias=1.0)
                    kb.op("act", "activation", [d_e], [d_e], out=e_[:], in_=e_[:], func=AF.Exp, scale=-1.0)
                    kb.op("dve", "tensor_tensor", [d_gsb, d_e], [d_gs[sl_][b]], out=gs[:, l0:l0 + 512], in0=g_sb[:], in1=e_[:], op=ALU.mult)
                    yield
                for gi in range(9):
                    tl0, ntl = (0, 2) if gi == 0 else (2 + 4 * (gi - 1), 4)
                    for ti in range(ntl):
                        tt = tl0 + ti
                        for kc in range(8):
                            kb.op("pe", "matmul", [dW, d_h[tt]], [d_pm], inc=(kc == 7), out=pm[:, ti * 128:(ti + 1) * 128],
                                  lhsT=hT[:, kc, tt * 128:(tt + 1) * 128], rhs=Wb[:, kc, 2, :], start=(kc == 0), stop=(kc == 7))
                        if ti < ntl - 1:
                            pst["mid"] = True
                            yield
                    pst["mid"] = False
                    kb.op("dve", "tensor_copy", [d_pm, d_Vone[sl_]], [d_Va[sl_][gi]], out=Va[:, tl0:tl0 + ntl, 0:128],
                          in_=pm[:, 0:ntl * 128].rearrange("p (a b) -> p a b", b=128))
                    yield

            def emit_S(h, qb, kbk):
                sl_ = h % NSLOT
                qT, kT = qT_s[sl_], kT_s[sl_]
                bk = 0 if kbk < 2 else 1 + (kbk - 2) // 4
                q0 = qb * 512
                ps, d_ps = psS.next()
                ksl = slice(kbk * 128, (kbk + 1) * 128)
                kb.op("pe", "matmul", [d_kT[sl_][bk], d_qT[sl_][qb]], [d_ps], inc=False, out=ps[:, 0:512], lhsT=kT[0:64, ksl],
                      rhs=qT[0:64, q0:q0 + 512], start=True, stop=True)
                kb.op("pe", "matmul", [d_kT[sl_][bk], d_qT[sl_][qb]], [d_ps], out=ps[:, 512:1024], lhsT=kT[64:128, ksl],
                      rhs=qT[64:128, q0:q0 + 512], start=True, stop=True)
                return ps, d_ps

            def emit_exp(ps, d_ps):
                PT, d_PT = PT_r.next()
                kb.op("act", "activation", [d_ps], [d_PT], out=PT[:], in_=ps[:, :], func=AF.Exp, scale=0.125)
                return PT, d_PT

            def emit_PV(h, qb, kbk, PT, d_PT):
                sl_ = h % NSLOT
                Va = Va_s[sl_]
                gk = 0 if kbk < 2 else 1 + (kbk - 2) // 4
                for half in range(2):
                    for qs in range(4):
                        i = half * 4 + qs
                        bank, slot = i // 3, i % 3
                        kb.op("pe", "matmul", [d_PT, d_Va[sl_][gk], d_Vone[sl_]], [d_acc], inc=(i == 7),
                              out=accb[bank][:, slot * 129:(slot + 1) * 129], lhsT=PT[:, half * 512 + qs * 128:half * 512 + (qs + 1) * 128],
                              rhs=Va[:, kbk, 0:129], start=(kbk == 0 and slot == 0), stop=(kbk == NT - 1), skip_group_check=True)

            def emit_readout(h, qb):
                sl_ = h % NSLOT
                gs = gs_s[sl_]
                q0 = qb * 512
                accs, d_accs = accs_r.next()
                af = accs[:].rearrange("p a b -> p (a b)")
                for bank in range(3):
                    cnt = 3 if bank < 2 else 2
                    kb.op("dve", "tensor_copy", [d_acc], [d_accs], out=af[:, bank * 387:bank * 387 + cnt * 129], in_=accb[bank][:, 0:cnt * 129])
                sm, d_sm = sm_r.next()
                kb.op("dve", "reciprocal", [d_accs], [d_sm], out=sm[:, 0:8], in_=accs[:, :, 128])
                kb.op("dve", "tensor_scalar", [d_sm, d_ls], [d_sm], out=sm[:, 8:12], in0=sm[:, 4:8], scalar1=ls[:, 3:4], scalar2=None, op0=ALU.mult)
                os_ = []
                for qs in range(4):
                    t_, d_t = t_r.next()
                    kb.op("dve", "tensor_scalar", [d_accs, d_sm], [d_t], out=t_[:], in0=accs[:, qs, 0:128], scalar1=sm[:, qs:qs + 1], scalar2=None,
                          op0=ALU.mult)
                    o_, d_o = o_r.next()
                    kb.op("dve", "scalar_tensor_tensor", [d_accs, d_sm, d_t], [d_o], out=o_[:], in0=accs[:, 4 + qs, 0:128],
                          scalar=sm[:, 8 + qs:9 + qs], in1=t_[:], op0=ALU.mult, op1=ALU.add)
                    kb.op("dve", "scalar_tensor_tensor", [d_o], [d_t, d_sm], out=t_[:], in0=o_[:], scalar=1.0, in1=o_[:], op0=ALU.mult, op1=ALU.mult,
                          accum_out=sm[:, 12 + qs:13 + qs])
                    os_.append((o_, d_o))
                ons = []

                def partB():
                    kb.op("act", "activation", [d_sm], [d_sm], out=sm[:, 12:16], in_=sm[:, 12:16], func=AF.Ln, scale=1.0 / 128, bias=EPS)
                    kb.op("act", "activation", [d_sm], [d_sm], out=sm[:, 12:16], in_=sm[:, 12:16], func=AF.Exp, scale=-0.5)
                    for qs in range(4):
                        o_, d_o = os_[qs]
                        on, d_on = on_r.next()
                        kb.op("dve", "tensor_scalar", [d_o, d_sm], [d_on], out=on[:], in0=o_[:], scalar1=sm[:, 12 + qs:13 + qs], scalar2=None,
                              op0=ALU.mult)
                        ons.append((on, d_on))

                def part2():
                    for qs in range(4):
                        on, d_on = ons[qs]
                        kb.op("pe", "transpose", [d_on, d_const], [d_pm], out=pm[:, qs * 128:(qs + 1) * 128], in_=on[:], identity=ident_t[:])
                    ofo, d_ofo = ofo_r.next()
                    kb.op("dve", "scalar_tensor_tensor", [d_pm, d_sg, d_gs[sl_][qb]], [d_ofo], out=ofo[:], in0=pm[:, :], scalar=sg[:, 1:2],
                          in1=gs[:, q0:q0 + 512], op0=ALU.mult, op1=ALU.mult)
                    kb.dma("pool", ofin1[h, :, q0:q0 + 512], ofo[:], d_ofo, reads=[d_ofo], writes=[d_of1])
                return partB, part2

            pg = proj_gen(0, 0)
            for _ in pg:
                pass
            its = [(h, qb, kbk) for h in range(heads1) for qb in range(8) for kbk in range(NT)]
            pg = None
            LOOK = 2
            NPSTEP = 104
            pg_done = [0]
            pend = []
            deferred = []
            started = set()

            def ensure_proj(hh):
                nonlocal pg
                if hh in started:
                    return
                if pg is not None:
                    for _ in pg:
                        pass
                    pg = None
                started.add(hh)

            started.add(0)
            for j in range(min(LOOK, len(its))):
                ensure_proj(its[j][0])
                pend.append(emit_S(*its[j]))
            for idx, (h, qb, kbk) in enumerate(its):
                if qb == 0 and kbk == 0:
                    if pg is not None:
                        for _ in pg:
                            pass
                    pg = proj_gen(h + 1, (h + 1) % NSLOT) if h + 1 < heads1 else None
                    pg_done[0] = 0
                PTd = emit_exp(*pend.pop(0))
                if idx + LOOK < len(its):
                    h2, qb2, kbk2 = its[idx + LOOK]
                    ensure_proj(h2)
                    pend.append(emit_S(h2, qb2, kbk2))
                emit_PV(h, qb, kbk, *PTd)
                if pg is not None:
                    it_h = qb * NT + kbk
                    need = min(NPSTEP, ((it_h + 1) * NPSTEP) // 250)
                    while pg_done[0] < need:
                        next(pg, None)
                        pg_done[0] += 1
                if kbk == NT - 1:
                    pB, p2 = emit_readout(h, qb)
                    deferred.append((idx + 9, pB))
                    deferred.append((idx + 15, p2))
                if deferred and (deferred[0][0] <= idx or idx == len(its) - 1):
                    while pst["mid"] and pg is not None:
                        if next(pg, "done") == "done":
                            break
                    while deferred and (deferred[0][0] <= idx or idx == len(its) - 1):
                        deferred.pop(0)[1]()
        kb.barrier()

        if dbg == "E":
            kb.final_wait([d_of1])
            return nc, kb

        d_out = Dep("out", acc=True)
        with contextlib.ExitStack() as es7:
            Wo = sbuf("Wo1", [128, 16, D], BF16, es7)
            d_Wo = Dep()
            wos_r = sb_rot(nc, es7, "wos1_", 2, [128, 2, D], F32)
            for k2 in range(8):
                ws, d_ws = wos_r.next()
                kb.dma("sp", ws[:], da_w_out[k2 * 256:(k2 + 1) * 256, :].rearrange("(k p) n -> p k n", p=128), d_ws, writes=[d_ws])
                kb.op("act", "activation", [d_ws], [d_Wo], out=Wo[:, k2 * 2:(k2 + 1) * 2, :], in_=ws[:], func=AF.Copy)
            gbc = sbuf("gbc1", [128, D], F32, es7)
            fg = sbuf("fg", [128, D], F32, es7)
            d_gbc = Dep(); d_fg = Dep()
            kb.dma("sp", gbc[:], gate_dram[:, 2, :], d_gbc, reads=[d_gdram], writes=[d_gbc])
            kb.dma("sp", fg[:], fgr[:, :], d_fg, writes=[d_fg])
            ofb_r = sb_rot(nc, es7, "ofb1_", 2, [128, 16, 512], BF16)
            xt_r = sb_rot(nc, es7, "xt2_", 3, [128, D], F32)
            xo_r = sb_rot(nc, es7, "xo_", 3, [128, D], F32)
            ot_r = sb_rot(nc, es7, "ot_", 2, [128, D], F32)
            junk = sbuf("junkF", [128, D], F32, es7)
            d_junk = Dep()
            st_r = sb_rot(nc, es7, "stF", 3, [128, 4], F32)
            psY = Rot([psum("psYF%d" % i, [128, 512], F32, es7) for i in range(4)], excl=True)
            pend_f = None
            for tt in range(L // 128):
                if tt % 4 == 0:
                    if tt == 0:
                        nxt_ofb = ofb_r.next()
                        kb.dma("sp", nxt_ofb[0][:], ofin1[:, :, 0:512].rearrange("h p t -> p h t"), nxt_ofb[1], reads=[d_of1], writes=[nxt_ofb[1]])
                    ofb, d_ofb = nxt_ofb
                    if tt + 4 < L // 128:
                        nxt_ofb = ofb_r.next()
                        kb.dma("sp", nxt_ofb[0][:], ofin1[:, :, (tt + 4) * 128:(tt + 4) * 128 + 512].rearrange("h p t -> p h t"), nxt_ofb[1],
                               reads=[d_of1], writes=[nxt_ofb[1]])
                off = (tt % 4) * 128
                xt, d_x = xt_r.next()
                kb.dma("sp", xt[:], x1[CTX + tt * 128:CTX + (tt + 1) * 128, :], d_x, reads=[d_x1], writes=[d_x])
                xo, d_xo = xo_r.next()
                for half in range(2):
                    py, d_py = psY.next()
                    hs = slice(half * 512, (half + 1) * 512)
                    for kc in range(16):
                        kb.op("pe", "matmul", [d_ofb, d_Wo], [d_py], inc=(kc == 15), out=py[:, :], lhsT=ofb[:, kc, off:off + 128],
                              rhs=Wo[:, kc, hs], start=(kc == 0), stop=(kc == 15))
                    kb.op("dve", "tensor_tensor", [d_py, d_gbc], [d_xo], out=xo[:, hs], in0=py[:, :], in1=gbc[:, hs], op=ALU.mult)
                    kb.op("dve", "tensor_tensor", [d_xo, d_x], [d_xo], out=xo[:, hs], in0=xo[:, hs], in1=xt[:, hs], op=ALU.add)
                st, d_st = st_r.next()
                kb.op("act", "activation", [d_xo], [d_junk, d_st], out=junk[:], in_=xo[:], func=AF.Square, accum_out=st[:, 0:1])
                kb.op("act", "activation", [d_st], [d_st], out=st[:, 1:2], in_=st[:, 0:1], func=AF.Ln, scale=1.0 / D, bias=EPS)
                kb.op("act", "activation", [d_st], [d_st], out=st[:, 2:3], in_=st[:, 1:2], func=AF.Exp, scale=-0.5)
                if pend_f is not None:
                    pend_f()

                def fin(tt=tt, xo=xo, d_xo=d_xo, st=st, d_st=d_st):
                    ot, d_ot = ot_r.next()
                    kb.op("dve", "scalar_tensor_tensor", [d_xo, d_st, d_fg], [d_ot], out=ot[:], in0=xo[:], scalar=st[:, 2:3], in1=fg[:],
                          op0=ALU.mult, op1=ALU.mult)
                    kb.dma("pool", out[tt * 128:(tt + 1) * 128, :], ot[:], d_ot, reads=[d_ot], writes=[d_out])
                pend_f = fin
            if pend_f is not None:
                pend_f()
        kb.final_wait([d_out])

    return nc, kb


def rope_tables():
    t = np.arange(L)
    row = (t // 64).astype(np.float64)
    col = (t % 64).astype(np.float64)
    inv = 1.0 / (10000.0 ** (np.arange(0, 32, 2, dtype=np.float64) / 32.0))
    f = np.arange(128)
    axis = (f % 64) // 32
    part = (f % 32) // 16
    i = f % 16
    pos = np.where(axis[:, None] == 0, row[None, :], col[None, :])
    ang = pos * inv[i][:, None]
    C = np.cos(ang)
    S = np.sin(ang) * np.where(part == 0, -1.0, 1.0)[:, None]
    import ml_dtypes
    return C.astype(np.float32).astype(ml_dtypes.bfloat16), S.astype(np.float32).astype(ml_dtypes.bfloat16)


def prep_inputs(b, x, c, ctx, c_ctx, w_ada, b_ada, norm_g, hg_w_in, hg_lb_logits, hg_norm_g, hg_w_out,
                da_w_in, da_lam_q1, da_lam_k1, da_lam_q2, da_lam_k2, da_subln_g, da_w_out, final_g, **kw):
    m = {}
    m["xin"] = np.ascontiguousarray(np.concatenate([ctx[b], x[b]], axis=0))
    cf = np.stack([c[b], c_ctx], axis=-1)
    m["cfm"] = np.ascontiguousarray(cf.reshape(8, 128, 2).transpose(1, 0, 2))
    m["w_ada"] = w_ada
    m["b_fm"] = np.ascontiguousarray(b_ada.reshape(2, 24, 128).transpose(0, 2, 1))
    m["b_row"] = np.ascontiguousarray(b_ada.reshape(2, 1, 3 * D))
    m["g_fm"] = np.ascontiguousarray(norm_g.reshape(2, 8, 128).transpose(0, 2, 1))
    m["ident"] = np.eye(128, dtype=np.float32)
    jj, tt = np.meshgrid(np.arange(128), np.arange(128), indexing="ij")
    m["maskf"] = (tt >= jj).astype(np.float32)
    m["maskb"] = (tt <= jj).astype(np.float32)
    m["hg_w_in"] = hg_w_in[0]
    m["lbl"] = np.ascontiguousarray(hg_lb_logits.reshape(2, 2, NH, 128).transpose(3, 0, 1, 2))
    m["hg_ng"] = np.ascontiguousarray(hg_norm_g[0].reshape(NH, 128).T)
    m["hg_w_out"] = hg_w_out[0]
    m["da_w_in"] = da_w_in[0]
    m["da_w_out"] = da_w_out[0]
    C, S = rope_tables()
    m["ropeC"] = C
    m["ropeS"] = S
    lv = np.stack([da_lam_q1[0], da_lam_k1[0], da_lam_q2[0], da_lam_k2[0]], axis=0)
    m["lamv"] = np.ascontiguousarray(np.broadcast_to(lv[None], (128, 4, 64))).astype(np.float32)
    m["subg"] = np.ascontiguousarray(da_subln_g[0].reshape(128, 1))
    pmat = np.zeros((128, 128), np.float32)
    pmat[np.arange(128) ^ 16, np.arange(128)] = 1.0
    m["permm"] = pmat
    m["fgr"] = np.ascontiguousarray(np.broadcast_to(final_g[None, :], (128, D))).astype(np.float32)
    return m


def kernel(**inputs):
    inputs = {k: np.asarray(v) for k, v in inputs.items()}
    nc, kb = build()
    in_maps = [prep_inputs(b, **inputs) for b in range(8)]
    res = run_bass_kernel_spmd(nc, in_maps, core_ids=list(range(8)))
    return np.stack([r["out"] for r in res.results], axis=0)
```

```python
import contextlib
import math
import numpy as np
import concourse.bass as bass
import concourse.mybir as mybir
from concourse.bass_utils import run_bass_kernel_spmd

F32 = mybir.dt.float32
BF16 = mybir.dt.bfloat16
AF = mybir.ActivationFunctionType
ALU = mybir.AluOpType

D = 1024
L = 4096
CTX = 256
T = CTX + L
NT = T // 128
DI = 2048
NH = 16
EPS = 1e-6
SAME_ENG_SYNC = True
BLOCKS = [(0, 256)] + [(256 + 512 * i, 512) for i in range(8)]
LAMBDA_INIT1 = 0.8 - 0.6 * math.exp(-0.3 * 1)


class Dep:
    __slots__ = ("w", "r", "dsem", "name", "excl", "acc")

    def __init__(self, name="", excl=False, acc=False):
        self.w = {}
        self.r = {}
        self.dsem = None
        self.name = name
        self.excl = excl
        self.acc = acc


class KB:
    def __init__(self, nc, es):
        self.nc = nc
        self.es = es
        self.eng = {"pe": nc.tensor, "act": nc.scalar, "dve": nc.vector, "pool": nc.gpsimd, "sp": nc.sync}
        self.sem = {}
        self.cnt = {}
        self.semobj = {}
        for k in ("pe", "act", "dve", "pool"):
            s = es.enter_context(nc.semaphore("s_" + k))
            self.sem[k] = s
            self.cnt[k] = 0
            self.semobj["E" + k] = s
        self.dtotal = {}
        self.seen = {k: {} for k in self.eng}
        self.ndsem = 0
        self.ninst = {k: 0 for k in self.eng}

    def _collect(self, e, reads, writes):
        ev = {}
        for d in reads:
            for k, v in d.w.items():
                if ev.get(k, 0) < v:
                    ev[k] = v
        own = "E" + e
        for d in writes:
            if d.acc:
                continue
            for k, v in d.w.items():
                if k == own:
                    continue
                if ev.get(k, 0) < v:
                    ev[k] = v
            for k, v in d.r.items():
                if ev.get(k, 0) < v:
                    ev[k] = v
        if own in ev and (e == "pe" or not SAME_ENG_SYNC):
            del ev[own]
        return ev

    def _wait(self, e, ev):
        seen = self.seen[e]
        for k, v in ev.items():
            if k in self.dtotal:
                v = self.dtotal[k]
            if seen.get(k, 0) < v:
                self.eng[e].wait_ge(self.semobj[k], v)
                self.ninst[e] += 1
                seen[k] = v

    def _mark(self, key, val, reads, writes):
        for d in reads:
            if d.r.get(key, 0) < val:
                d.r[key] = val
        for d in writes:
            if d.acc:
                if d.w.get(key, 0) < val:
                    d.w[key] = val
                continue
            d.w = {key: val}
            d.r = {}

    def op(self, e, meth, reads, writes, inc=True, **kw):
        if any(d.excl for d in reads):
            writes = list(writes) + [d for d in reads if d.excl]
            reads = [d for d in reads if not d.excl]
        ev = self._collect(e, reads, writes)
        self._wait(e, ev)
        ins = getattr(self.eng[e], meth)(**kw)
        self.ninst[e] += 1
        if inc:
            self.cnt[e] += 1
            ins.then_inc(self.sem[e], 1)
            val = self.cnt[e]
        else:
            val = self.cnt[e] + 1
        self._mark("E" + e, val, reads, writes)
        return ins

    def dma(self, q, out, in_, sb, reads=(), writes=(), **kw):
        ev = self._collect(q, reads, writes)
        self._wait(q, ev)
        if sb.dsem is None:
            self.ndsem += 1
            name = "d%d" % self.ndsem
            s = self.es.enter_context(self.nc.semaphore(name))
            sb.dsem = name
            self.semobj[name] = s
            self.dtotal[name] = 0
        key = sb.dsem
        ins = self.eng[q].dma_start(out=out, in_=in_, **kw)
        self.ninst[q] += 1
        self.dtotal[key] += 16
        ins.then_inc(self.semobj[key], 16)
        self._mark(key, self.dtotal[key], reads, writes)
        return ins

    def barrier(self):
        ev = {"E" + k: v for k, v in self.cnt.items() if v > 0}
        for k, v in self.dtotal.items():
            if v > 0:
                ev[k] = v
        for e in self.eng:
            mine = dict(ev)
            mine.pop("E" + e, None)
            self._wait(e, mine)

    def final_wait(self, deps):
        ev = {}
        for d in deps:
            for k, v in d.w.items():
                ev[k] = max(ev.get(k, 0), v)
        self._wait("sp", ev)


class Rot:
    def __init__(self, bufs, excl=False):
        self.bufs = [(b, Dep(excl=excl)) for b in bufs]
        self.i = 0

    def next(self):
        b = self.bufs[self.i % len(self.bufs)]
        self.i += 1
        return b


def sb_rot(nc, es, name, n, shape, dt):
    return Rot([es.enter_context(nc.sbuf_tensor("%s%d" % (name, i), list(shape), dt)) for i in range(n)])


def build(dbg=None, heads0=NH, heads1=NH, skip0=False, eopt=""):
    nc = bass.Bass("TRN2", target_bir_lowering=False)

    def din(name, shape, dt=F32):
        return nc.dram_tensor(name, list(shape), dt, kind="ExternalInput").ap()

    xin = din("xin", [T, D])
    cfm = din("cfm", [128, 8, 2])
    w_ada = din("w_ada", [2, D, 3 * D])
    b_fm = din("b_fm", [2, 128, 24])
    b_row = din("b_row", [2, 1, 3 * D])
    g_fm = din("g_fm", [2, 128, 8])
    ident = din("ident", [128, 128])
    maskf = din("maskf", [128, 128])
    maskb = din("maskb", [128, 128])
    hg_w_in = din("hg_w_in", [D, 5 * DI])
    lbl = din("lbl", [128, 2, 2, NH])
    hg_ng = din("hg_ng", [128, NH])
    hg_w_out = din("hg_w_out", [DI, D])
    da_w_in = din("da_w_in", [D, 4 * DI])
    da_w_out = din("da_w_out", [DI, D])
    ropeC = din("ropeC", [128, L], BF16)
    ropeS = din("ropeS", [128, L], BF16)
    lamv = din("lamv", [128, 4, 64])
    subg = din("subg", [128, 1])
    fgr = din("fgr", [128, D])
    permm = din("permm", [128, 128])
    out = nc.dram_tensor("out", [L, D], F32, kind="ExternalOutput").ap()
    ofin1 = nc.dram_tensor("ofin1", [NH, 128, L], BF16, kind="ExternalOutput" if dbg == "E" else "Internal").ap()
    okind = "ExternalOutput" if dbg == "C" else "Internal"
    ofin0 = nc.dram_tensor("ofin0", [NH, 128, T], BF16, kind=okind).ap()
    x1 = nc.dram_tensor("x1", [T, D], F32, kind="ExternalOutput" if dbg == "D" else "Internal").ap()

    with contextlib.ExitStack() as es:
        kb = KB(nc, es)

        def sbuf(name, shape, dt, stack=es):
            return stack.enter_context(nc.sbuf_tensor(name, list(shape), dt))

        dumps = {}

        def dump(name, ap, deps, dt=F32):
            if dbg != "C":
                return
            if name in dumps:
                return
            import os
            sel = os.environ.get("DUMPS", "")
            if sel and not any(name.startswith(x) for x in sel.split(",")):
                return
            shp = list(ap.shape)
            dten = nc.dram_tensor("dump_" + name, shp, dt, kind="ExternalOutput").ap()
            dd_ = Dep()
            dumps[name] = dd_
            kb.dma("sp", dten, ap, dd_, reads=deps, writes=[dd_])

        def psum(name, shape, dt, stack=es):
            return stack.enter_context(nc.psum_tensor(name, list(shape), dt))

        ident_t = sbuf("ident_t", [128, 128], F32)
        ident_b = sbuf("ident_b", [128, 128], BF16)
        ones_t = sbuf("ones_t", [128, 128], F32)
        ones_b = sbuf("ones_b", [128, 128], BF16)
        mf_t = sbuf("mf_t", [128, 128], F32)
        mb_t = sbuf("mb_t", [128, 128], F32)
        d_const = Dep("const")
        kb.dma("sp", ident_t[:], ident[:, :], d_const, writes=[d_const])
        kb.dma("sp", mf_t[:], maskf[:, :], d_const, writes=[d_const])
        kb.dma("sp", mb_t[:], maskb[:, :], d_const, writes=[d_const])
        d_ones = Dep("ones")
        kb.op("pool", "memset", [], [d_ones], ap=ones_t[:], constant=1.0)
        kb.op("pool", "memset", [], [d_ones], ap=ones_b[:], constant=1.0)
        d_identb = Dep()
        kb.op("dve", "tensor_copy", [d_const], [d_identb], out=ident_b[:], in_=ident_t[:])

        aT = sbuf("aT", [128, 2, 8, 2], F32)
        shT = sbuf("shT", [128, 2, 8, 2], F32)
        gate_dram = nc.dram_tensor("gate_dram", [128, 3, D], F32, kind="Internal").ap()
        d_gdram = Dep("gdram")
        d_aT = Dep("aT"); d_shT = Dep("shT"); d_gate = [Dep("g0"), Dep("g1"), Dep("g2")]
        with contextlib.ExitStack() as es2:
            gate_bc = sbuf("gate_bc", [128, 3, D], F32, es2)
            sv = sbuf("sv", [128, 8, 2], F32, es2)
            sc = sbuf("sc", [128, 8, 2], F32, es2)
            th = sbuf("th", [128, 8, 2], F32, es2)
            wada = sbuf("wada", [128, 8, 3 * D], F32, es2)
            bfm = sbuf("bfm", [128, 2, 24], F32, es2)
            gfm = sbuf("gfm", [128, 2, 8], F32, es2)
            brow = sbuf("brow", [1, 3 * D], F32, es2)
            scbc = sbuf("scbc", [128, 8, 2, 128], F32, es2)
            modT = sbuf("modT", [128, 24, 2], F32, es2)
            psA = psum("psA", [128, 256, 2], F32, es2)
            psG = psum("psG", [128, 2, 512], F32, es2)
            d_sv = Dep(); d_sc = Dep(); d_th = Dep(); d_w = Dep(); d_bfm = Dep(); d_gfm = Dep(); d_brow = Dep()
            d_scbc = Dep(); d_modT = Dep(); d_psA = Dep(excl=True); d_psG = [Dep(excl=True), Dep(excl=True)]
            kb.dma("sp", sv[:], cfm[:, :, :], d_sv, writes=[d_sv])
            kb.dma("sp", bfm[:], b_fm.rearrange("l p j -> p l j"), d_bfm, writes=[d_bfm])
            kb.dma("sp", gfm[:], g_fm.rearrange("l p j -> p l j"), d_gfm, writes=[d_gfm])
            kb.op("act", "activation", [d_sv], [d_th], out=th[:], in_=sv[:], func=AF.Tanh, scale=0.5)
            kb.op("dve", "tensor_scalar", [d_th], [d_th], out=th[:], in0=th[:], scalar1=0.5, scalar2=0.5, op0=ALU.mult, op1=ALU.add)
            kb.op("dve", "tensor_mul", [d_th, d_sv], [d_sc], out=sc[:], in0=th[:], in1=sv[:])
            for kc in range(8):
                for s in range(2):
                    kb.op("dve", "tensor_scalar", [d_sc, d_ones], [d_scbc], out=scbc[:, kc, s, :], in0=ones_t[:],
                          scalar1=sc[:, kc, s:s + 1], scalar2=None, op0=ALU.mult)
            for l in range(2):
                kb.dma("sp", brow[:], b_row[l, :, :], d_brow, writes=[d_brow])
                for half in range(2):
                    kb.dma("sp", wada[:, half * 4:(half + 1) * 4, :],
                           w_ada[l, half * 512:(half + 1) * 512, :].rearrange("(kc p) n -> p kc n", p=128), d_w, writes=[d_w])
                for j in range(16):
                    for kc in range(8):
                        kb.op("pe", "matmul", [d_w, d_sc], [d_psA], inc=(j == 15 and kc == 7),
                              out=psA[:, j, :], lhsT=wada[:, kc, j * 128:(j + 1) * 128], rhs=sc[:, kc, :], start=(kc == 0), stop=(kc == 7))
                kb.op("dve", "tensor_tensor", [d_psA, d_bfm], [d_modT], out=modT[:, 0:16, :], in0=psA[:, 0:16, :],
                      in1=bfm[:, l, 0:16].unsqueeze(2).to_broadcast([128, 16, 2]), op=ALU.add)
                kb.op("dve", "tensor_copy", [d_modT], [d_shT], out=shT[:, l, :, :], in_=modT[:, 0:8, :])
                kb.op("dve", "scalar_tensor_tensor", [d_modT, d_gfm], [d_aT], out=aT[:, l, :, :], in0=modT[:, 8:16, :], scalar=1.0,
                      in1=gfm[:, l, :].unsqueeze(2).to_broadcast([128, 8, 2]), op0=ALU.add, op1=ALU.mult)
                for s in range(2):
                    if l == 1 and s == 1:
                        continue
                    gi = l * 2 + s
                    for nb in range(2):
                        c0 = 2048 + nb * 512
                        for kc in range(8):
                            kb.op("pe", "matmul", [d_scbc, d_w], [d_psG[nb]], inc=False,
                                  out=psG[:, nb, :], lhsT=scbc[:, kc, s, :], rhs=wada[:, kc, c0:c0 + 512], start=(kc == 0), stop=False)
                        kb.op("pe", "matmul", [d_ones, d_brow], [d_psG[nb]],
                              out=psG[:, nb, :], lhsT=ones_t[0:1, :], rhs=brow[0:1, c0:c0 + 512], start=False, stop=True)
                        kb.op("act", "activation", [d_psG[nb]], [d_gate[gi]],
                              out=gate_bc[:, gi, nb * 512:(nb + 1) * 512], in_=psG[:, nb, :], func=AF.Copy)
            kb.dma("sp", gate_dram[:, :, :], gate_bc[:], d_gate[0], reads=d_gate, writes=[d_gdram])

        kb.barrier()
        hT = sbuf("hT", [128, 8, T], BF16)
        d_h = [Dep("h%d" % i) for i in range(NT)]

        def prestage(l, es3, psT, d_psT, tiles, get_x):
            junk = sbuf("junk%d" % l, [128, D], F32, es3)
            d_junk = Dep()
            xn_r = sb_rot(nc, es3, "xn%d_" % l, 3, [128, D], F32)
            st_r = sb_rot(nc, es3, "st%d_" % l, 4, [128, 4], F32)
            def stage_a(tt):
                xt, d_x = get_x(tt)
                st, d_st = st_r.next()
                kb.op("act", "activation", [d_x], [d_junk, d_st], out=junk[:], in_=xt, func=AF.Square, accum_out=st[:, 0:1])
                kb.op("act", "activation", [d_st], [d_st], out=st[:, 1:2], in_=st[:, 0:1], func=AF.Ln, scale=1.0 / D, bias=EPS)
                kb.op("act", "activation", [d_st], [d_st], out=st[:, 2:3], in_=st[:, 1:2], func=AF.Exp, scale=-0.5)
                xn, d_xn = xn_r.next()
                kb.op("dve", "tensor_scalar", [d_x, d_st], [d_xn], out=xn[:], in0=xt, scalar1=st[:, 2:3], scalar2=None, op0=ALU.mult)
                return xn, d_xn

            def stage_b(tt, xn, d_xn):
                s = 1 if tt < 2 else 0
                for hb in range(2):
                    for c in range(4):
                        cc = hb * 4 + c
                        kb.op("pe", "transpose", [d_xn, d_const], [d_psT[hb]], inc=(c == 3),
                              out=psT[:, hb, c, :], in_=xn[:, cc * 128:(cc + 1) * 128], identity=ident_t[:])
                    for c in range(4):
                        cc = hb * 4 + c
                        kb.op("act", "activation", [d_psT[hb], d_aT, d_shT], [d_h[tt]],
                              out=hT[:, cc, tt * 128:(tt + 1) * 128], in_=psT[:, hb, c, :], func=AF.Identity,
                              scale=aT[:, l, cc, s:s + 1], bias=shT[:, l, cc, s:s + 1])
            tiles = list(tiles)
            cur_ = stage_a(tiles[0])
            for i_, tt in enumerate(tiles):
                nxt_ = stage_a(tiles[i_ + 1]) if i_ + 1 < len(tiles) else None
                stage_b(tt, *cur_)
                cur_ = nxt_

        with contextlib.ExitStack() as es3:
            psT = psum("psT0", [128, 2, 4, 128], F32, es3)
            d_psT = [Dep(excl=True), Dep(excl=True)]
            xt_r = sb_rot(nc, es3, "xt0_", 3, [128, D], F32)

            def get_x0(tt):
                xt, d_x = xt_r.next()
                kb.dma("sp", xt[:], xin[tt * 128:(tt + 1) * 128, :], d_x, writes=[d_x])
                return xt[:], d_x
            prestage(0, es3, psT, d_psT, range(NT), get_x0)
        kb.barrier()

        if dbg == "B":
            dbg_out = nc.dram_tensor("dbg", [128, 8, T], BF16, kind="ExternalOutput").ap()
            d_dbg = Dep()
            kb.dma("sp", dbg_out[:, :, :], hT[:], d_dbg, reads=d_h, writes=[d_dbg])
            kb.final_wait([d_dbg])
            return nc, kb

        d_of0 = Dep("ofin0", acc=True)
        with contextlib.ExitStack() as es4:
            lg = sbuf("lg", [128, 2, 2, NH], F32, es4)
            cA = sbuf("cA", [128, 2, NH], F32, es4)
            cB = sbuf("cB", [128, 2, NH], F32, es4)
            ncA = sbuf("ncA", [128, 2, NH], F32, es4)
            ngh = sbuf("ngh", [128, NH], F32, es4)
            tmpc = sbuf("tmpc", [128, 2, NH], F32, es4)
            d_lg = Dep(); d_cc = Dep(); d_ngh = Dep(); d_tmpc = Dep()
            kb.dma("sp", lg[:], lbl[:, :, :, :], d_lg, writes=[d_lg])
            kb.dma("sp", ngh[:], hg_ng[:, :], d_ngh, writes=[d_ngh])
            kb.op("dve", "tensor_scalar", [d_ngh], [d_ngh], out=ngh[:], in0=ngh[:], scalar1=0.5, scalar2=None, op0=ALU.mult)
            kb.op("dve", "tensor_tensor", [d_lg], [d_tmpc], out=tmpc[:], in0=lg[:, 1, :, :], in1=lg[:, 0, :, :], op=ALU.subtract)
            kb.op("act", "activation", [d_tmpc], [d_tmpc], out=tmpc[:], in_=tmpc[:], func=AF.Exp)
            kb.op("dve", "tensor_scalar", [d_tmpc], [d_tmpc], out=tmpc[:], in0=tmpc[:], scalar1=1.0, scalar2=None, op0=ALU.add)
            kb.op("dve", "reciprocal", [d_tmpc], [d_tmpc], out=tmpc[:], in_=tmpc[:])
            kb.op("dve", "tensor_scalar", [d_tmpc], [d_cc], out=cA[:], in0=tmpc[:], scalar1=-0.5, scalar2=0.5, op0=ALU.mult, op1=ALU.add)
            kb.op("dve", "tensor_scalar", [d_tmpc], [d_cc], out=cB[:], in0=tmpc[:], scalar1=0.5, scalar2=0.5, op0=ALU.mult, op1=ALU.add)
            kb.op("dve", "tensor_scalar", [d_tmpc], [d_cc], out=ncA[:], in0=tmpc[:], scalar1=0.5, scalar2=-0.5, op0=ALU.mult, op1=ALU.add)

            wst_r = sb_rot(nc, es4, "wst", 1, [128, 8, 128], F32)
            Wh_r = sb_rot(nc, es4, "Wh", 1, [128, 8, 5, 128], BF16)
            qs2 = sbuf("qs2", [128, T], BF16, es4)
            gs2 = sbuf("gs2", [128, T], BF16, es4)
            thz = [sbuf("thzf", [128, T], F32, es4), sbuf("thzb", [128, T], F32, es4)]
            v_all = sbuf("v_all", [128, NT, 128], BF16, es4)
            ofw = sbuf("ofw", [128, T], BF16, es4)
            NB = len(BLOCKS)
            d_qs2 = [Dep() for _ in range(NB)]; d_gs2 = [Dep() for _ in range(NB)]
            d_thz = [[Dep() for _ in range(NB)] for _ in range(2)]
            d_v = [Dep() for _ in range(NB)]; d_ofw = [Dep() for _ in range(NB)]
            f32_r = sb_rot(nc, es4, "tf", 13, [128, 512], F32)
            th_r = sb_rot(nc, es4, "tht", 2, [128, 512], F32)
            fin_r = sb_rot(nc, es4, "finr", 2, [128, 512], F32)
            b16_r = sb_rot(nc, es4, "tb", 10, [128, 512], BF16)
            sm_r = sb_rot(nc, es4, "sm", 4, [128, 8], F32)
            aTf_r = sb_rot(nc, es4, "aTf", 2, [128, 4, 128], BF16)
            aTb_r = sb_rot(nc, es4, "aTb", 2, [128, 4, 128], BF16)
            for r_ in (aTf_r, aTb_r):
                for (b_, d_) in r_.bufs:
                    kb.op("dve", "memset", [], [d_], ap=b_[:], constant=0.0)
            kdT_r = sb_rot(nc, es4, "kdT", 2, [128, 4, 128], BF16)
            Sb_r = sb_rot(nc, es4, "Sb", 2, [128, 128], BF16)
            S32 = sbuf("S32", [128, 128], F32, es4)
            d_S32 = Dep()
            of_r = sb_rot(nc, es4, "ofo", 2, [128, 512], BF16)
            psP = Rot([psum("psP%d" % i, [128, 512], F32, es4) for i in range(3)], excl=True)
            psO = Rot([psum("psO%d" % i, [128, 512], F32, es4) for i in range(2)], excl=True)
            psA_r = Rot([psum("psAa", [128, 512], F32, es4)], excl=True)
            psU_r = Rot([psum("psUu", [128, 512], F32, es4)], excl=True)
            psK_r = Rot([psum("psKk", [128, 1024], BF16, es4)], excl=True)

            def load_w(h):
                Wh, d_W = Wh_r.next()
                for wi in range(5):
                    wst, d_wst = wst_r.next()
                    c0 = wi * DI + h * 128
                    kb.dma("sp", wst[:], hg_w_in[:, c0:c0 + 128].rearrange("(kc p) j -> p kc j", p=128), d_wst, writes=[d_wst])
                    kb.op("act", "activation", [d_wst], [d_W], out=Wh[:, :, wi, :], in_=wst[:], func=AF.Copy)
                return Wh, d_W

            def stage1_gen(h, Wh, d_W, blocks=None):
                for bi, (t0, n) in enumerate(BLOCKS):
                    if blocks is not None and bi not in blocks:
                        continue
                    hd = d_h[t0 // 128:(t0 + n) // 128]
                    for wi in (0, 4, 1, 2):
                        ps, d_ps = psP.next()
                        for kc in range(8):
                            kb.op("pe", "matmul", [d_W] + hd, [d_ps], inc=(kc == 7), out=ps[:, 0:n], lhsT=Wh[:, kc, wi, :],
                                  rhs=hT[:, kc, t0:t0 + n], start=(kc == 0), stop=(kc == 7))
                        if wi in (0, 4):
                            tht, d_tht = th_r.next()
                            dst, d_dst = (qs2, d_qs2[bi]) if wi == 0 else (gs2, d_gs2[bi])
                            kb.op("act", "activation", [d_ps], [d_tht], out=tht[:, 0:n], in_=ps[:, 0:n], func=AF.Tanh, scale=0.5)
                            kb.op("dve", "scalar_tensor_tensor", [d_tht, d_ps], [d_dst], out=dst[:, t0:t0 + n], in0=tht[:, 0:n], scalar=1.0,
                                  in1=ps[:, 0:n], op0=ALU.add, op1=ALU.mult)
                        else:
                            di = wi - 1
                            kb.op("act", "activation", [d_ps], [d_thz[di][bi]], out=thz[di][:, t0:t0 + n], in_=ps[:, 0:n], func=AF.Tanh, scale=0.5)
                        yield
                    ps, d_ps = psP.next()
                    ncx = n // 128
                    for ci in range(ncx):
                        for kc in range(8):
                            kb.op("pe", "matmul", [d_W] + hd, [d_ps], inc=(kc == 7 and ci == ncx - 1), out=ps[:, ci * 128:(ci + 1) * 128],
                                  lhsT=hT[:, kc, t0 + ci * 128:t0 + (ci + 1) * 128], rhs=Wh[:, kc, 3, :], start=(kc == 0), stop=(kc == 7))
                    kb.op("act", "activation", [d_ps], [d_v[bi]], out=v_all[:, t0 // 128:(t0 + n) // 128, :].rearrange("p a b -> p (a b)"),
                          in_=ps[:, 0:n], func=AF.Copy)
                    yield

            def prep_gen(h, di, bi, P):
                t0, n = BLOCKS[bi]
                ncx = n // 128
                ref = 63 if di == 0 else 64
                last = 127 if di == 0 else 0
                lf, d_lf = f32_r.next()
                kb.op("act", "activation", [d_thz[di][bi], d_cc], [d_lf], out=lf[:, 0:n], in_=thz[di][:, t0:t0 + n], func=AF.Ln,
                      scale=cA[:, di, h:h + 1], bias=cB[:, di, h:h + 1])
                kk, d_kk = f32_r.next()
                kb.op("act", "activation", [d_thz[di][bi], d_cc], [d_kk], out=kk[:, 0:n], in_=thz[di][:, t0:t0 + n], func=AF.Identity,
                      scale=ncA[:, di, h:h + 1], bias=cA[:, di, h:h + 1])
                yield
                pre, d_pre = f32_r.next()
                for ci in range(ncx):
                    kb.op("dve", "tensor_tensor_scan", [d_lf, d_ones], [d_pre], out=pre[:, ci * 128:(ci + 1) * 128], data0=ones_t[:, 0:128],
                          data1=lf[:, ci * 128:(ci + 1) * 128], initial=0.0, op0=ALU.mult, op1=ALU.add)
                    if ci % 2 == 1 and ci < ncx - 1:
                        yield
                if di == 0:
                    dd, d_dd = pre, d_pre
                else:
                    dd, d_dd = f32_r.next()
                    kb.op("dve", "tensor_tensor", [d_lf, d_pre], [d_dd], out=dd[:, 0:n], in0=lf[:, 0:n], in1=pre[:, 0:n], op=ALU.subtract)
                sm, d_sm = sm_r.next()
                kb.op("dve", "tensor_scalar", [d_dd], [d_sm], out=sm[:, 0:ncx], in0=dd[:, 0:n].rearrange("p (c t) -> p c t", t=128)[:, :, ref],
                      scalar1=-1.0, scalar2=None, op0=ALU.mult)
                yield
                E, d_E = f32_r.next(); Er, d_Er = f32_r.next(); Ek, d_Ek = f32_r.next(); Ed, d_Ed = f32_r.next()
                for ci in range(ncx):
                    sl = slice(ci * 128, (ci + 1) * 128)
                    if di == 0:
                        kb.op("act", "activation", [d_dd], [d_E], out=E[:, sl], in_=dd[:, sl], func=AF.Exp)
                    else:
                        kb.op("act", "activation", [d_dd, d_pre], [d_E], out=E[:, sl], in_=dd[:, sl], func=AF.Exp,
                              bias=pre[:, ci * 128 + 127:ci * 128 + 128])
                    kb.op("act", "activation", [d_dd, d_sm], [d_Er], out=Er[:, sl], in_=dd[:, sl], func=AF.Exp, bias=sm[:, ci:ci + 1])
                    kb.op("act", "activation", [d_dd], [d_Ek], out=Ek[:, sl], in_=dd[:, sl], func=AF.Exp, scale=-1.0,
                          bias=dd[:, ci * 128 + ref:ci * 128 + ref + 1])
                    kb.op("act", "activation", [d_dd], [d_Ed], out=Ed[:, sl], in_=dd[:, sl], func=AF.Exp, scale=-1.0,
                          bias=dd[:, ci * 128 + last:ci * 128 + last + 1])
                    yield
                qe, d_qe = b16_r.next(); qr, d_qr = b16_r.next(); kr, d_kr = b16_r.next(); kd, d_kd = b16_r.next()
                kb.op("dve", "tensor_tensor", [d_qs2[bi], d_E], [d_qe], out=qe[:, 0:n], in0=qs2[:, t0:t0 + n], in1=E[:, 0:n], op=ALU.mult)
                kb.op("dve", "tensor_tensor", [d_qs2[bi], d_Er], [d_qr], out=qr[:, 0:n], in0=qs2[:, t0:t0 + n], in1=Er[:, 0:n], op=ALU.mult)
                yield
                kb.op("dve", "tensor_tensor", [d_kk, d_Ek], [d_kr], out=kr[:, 0:n], in0=kk[:, 0:n], in1=Ek[:, 0:n], op=ALU.mult)
                kb.op("dve", "tensor_tensor", [d_kk, d_Ed], [d_kd], out=kd[:, 0:n], in0=kk[:, 0:n], in1=Ed[:, 0:n], op=ALU.mult)
                P.update(E=E, d_E=d_E, qe=qe, d_qe=d_qe, qr=qr, d_qr=d_qr, kr=kr, d_kr=d_kr, kd=kd, d_kd=d_kd)

            state = {}

            def chunks(h, di, bi, P, pgen=None, sgen=None):
                t0, n = BLOCKS[bi]
                ncx = n // 128
                last = 127 if di == 0 else 0
                mask_t = mf_t if di == 0 else mb_t
                aT_r = aTf_r if di == 0 else aTb_r
                E, d_E, qe, d_qe, qr, d_qr, kr, d_kr, kd, d_kd = (P[k] for k in ("E", "d_E", "qe", "d_qe", "qr", "d_qr", "kr", "d_kr", "kd", "d_kd"))

                def step(k=1):
                    for _ in range(k):
                        if pgen is not None:
                            next(pgen, None)
                        if sgen is not None:
                            next(sgen, None)
                pa, d_pa = psA_r.next()
                pk, d_pk = psK_r.next()
                for ci in range(ncx):
                    sl = slice(ci * 128, (ci + 1) * 128)
                    kb.op("pe", "matmul", [d_kr, d_qr], [d_pa], inc=(ci == ncx - 1), out=pa[:, sl], lhsT=kr[:, sl], rhs=qr[:, sl], start=True, stop=True,
                          skip_group_check=True)
                for ci in range(ncx):
                    sl = slice(ci * 128, (ci + 1) * 128)
                    kb.op("pe", "transpose", [d_kd, d_identb], [d_pk], inc=(ci == ncx - 1), out=pk[:, sl], in_=kd[:, sl], identity=ident_b[:])
                step(2 if sgen is not None else 1)
                aS, d_aS = aT_r.next()
                pa3 = pa[:, 0:n].rearrange("p (c t) -> p c t", t=128)
                if di == 0:
                    kb.op("dve", "tensor_tensor", [d_pa, d_const], [d_aS], out=aS[0:64, 0:ncx, :], in0=pa3[0:64, :, :],
                          in1=mask_t[0:64, :].unsqueeze(1).to_broadcast([64, ncx, 128]), op=ALU.mult)
                    kb.op("dve", "tensor_tensor", [d_pa, d_const], [d_aS], out=aS[64:128, 0:ncx, 64:128], in0=pa3[64:128, :, 64:128],
                          in1=mask_t[64:128, 64:128].unsqueeze(1).to_broadcast([64, ncx, 64]), op=ALU.mult)
                else:
                    kb.op("dve", "tensor_tensor", [d_pa, d_const], [d_aS], out=aS[0:64, 0:ncx, 0:64], in0=pa3[0:64, :, 0:64],
                          in1=mask_t[0:64, 0:64].unsqueeze(1).to_broadcast([64, ncx, 64]), op=ALU.mult)
                    kb.op("dve", "tensor_tensor", [d_pa, d_const], [d_aS], out=aS[64:128, 0:ncx, :], in0=pa3[64:128, :, :],
                          in1=mask_t[64:128, :].unsqueeze(1).to_broadcast([64, ncx, 128]), op=ALU.mult)
                kT_, d_kT = kdT_r.next()
                kb.op("act", "activation", [d_pk], [d_kT], out=kT_[:, 0:ncx, :].rearrange("p c t -> p (c t)"), in_=pk[:, 0:n], func=AF.Copy)
                step(2 if sgen is not None else 1)
                pu, d_pu = psU_r.next()
                po, d_po = psO.next()
                for ci in range(ncx):
                    sl = slice(ci * 128, (ci + 1) * 128)
                    tile = t0 // 128 + ci
                    kb.op("pe", "matmul", [d_kT, d_v[bi]], [d_pu], inc=(ci == ncx - 1), out=pu[:, sl], lhsT=kT_[:, ci, :], rhs=v_all[:, tile, :],
                          start=True, stop=True, skip_group_check=True)
                for ci in range(ncx):
                    sl = slice(ci * 128, (ci + 1) * 128)
                    tile = t0 // 128 + ci
                    kb.op("pe", "matmul", [d_v[bi], d_aS], [d_po], inc=False, out=po[:, sl], lhsT=v_all[:, tile, :], rhs=aS[:, ci, :],
                          start=(ci == 0), stop=False, skip_group_check=True)
                step(2 if sgen is not None else 1)
                order_c = list(range(ncx)) if di == 0 else list(range(ncx - 1, -1, -1))
                for ci in order_c:
                    sl = slice(ci * 128, (ci + 1) * 128)
                    Sb, d_Sb = state["Sb"]
                    kb.op("pe", "matmul", [d_Sb, d_qe], [d_po], out=po[:, sl], lhsT=Sb[:], rhs=qe[:, sl], start=False, stop=True, skip_group_check=True)
                    dl = ci * 128 + last
                    kb.op("dve", "scalar_tensor_tensor", [d_E, d_pu, d_S32], [d_S32], out=S32[:], in0=S32[:], scalar=E[:, dl:dl + 1], in1=pu[:, sl],
                          op0=ALU.mult, op1=ALU.add)
                    Sb, d_Sb = Sb_r.next()
                    kb.op("dve", "tensor_copy", [d_S32], [d_Sb], out=Sb[:], in_=S32[:])
                    state["Sb"] = (Sb, d_Sb)
                    step()
                return po, d_po

            def finish_gen(h, di, bi, po, d_po):
                t0, n = BLOCKS[bi]
                if di == 0:
                    kb.op("act", "activation", [d_po], [d_ofw[bi]], out=ofw[:, t0:t0 + n], in_=po[:, 0:n], func=AF.Copy)
                    return
                ot, d_ot = fin_r.next()
                kb.op("dve", "tensor_tensor", [d_po, d_ofw[bi]], [d_ot], out=ot[:, 0:n], in0=po[:, 0:n], in1=ofw[:, t0:t0 + n], op=ALU.add)
                yield
                osq, d_osq = b16_r.next()
                kb.op("act", "activation", [d_ot], [d_osq], out=osq[:, 0:n], in_=ot[:, 0:n], func=AF.Square)
                pss, d_pss = psP.next()
                kb.op("pe", "matmul", [d_ones, d_osq], [d_pss], out=pss[:, 0:n], lhsT=ones_b[:], rhs=osq[:, 0:n], start=True, stop=True)
                yield
                rs, d_rs = fin_r.next()
                kb.op("act", "activation", [d_pss], [d_rs], out=rs[:, 0:n], in_=pss[:, 0:n], func=AF.Ln, scale=1.0 / 128, bias=EPS * 4)
                kb.op("act", "activation", [d_rs], [d_rs], out=rs[:, 0:n], in_=rs[:, 0:n], func=AF.Exp, scale=-0.5)
                yield
                kb.op("dve", "tensor_tensor", [d_rs, d_ot], [d_ot], out=ot[:, 0:n], in0=ot[:, 0:n], in1=rs[:, 0:n], op=ALU.mult)
                of_, d_of = of_r.next()
                kb.op("dve", "scalar_tensor_tensor", [d_gs2[bi], d_ngh, d_ot], [d_of], out=of_[:, 0:n], in0=gs2[:, t0:t0 + n],
                      scalar=ngh[:, h:h + 1], in1=ot[:, 0:n], op0=ALU.mult, op1=ALU.mult)
                kb.dma("pool", ofin0[h, :, t0:t0 + n], of_[:, 0:n], d_of, reads=[d_of], writes=[d_of0])

            def seq_gen(*gens):
                for g_ in gens:
                    if g_ is not None:
                        for _ in g_:
                            yield

            Wcur = load_w(0) if heads0 > 0 else None
            if heads0 > 0:
                for _ in stage1_gen(0, *Wcur):
                    pass
            for h in range(heads0):
                Wn = load_w(h + 1) if h + 1 < heads0 else None
                for di in range(2):
                    kb.op("dve", "memset", [], [d_S32], ap=S32[:], constant=0.0)
                    Sb, d_Sb = Sb_r.next()
                    kb.op("dve", "memset", [], [d_Sb], ap=Sb[:], constant=0.0)
                    state["Sb"] = (Sb, d_Sb)
                    order = list(range(NB)) if di == 0 else [0] + list(range(NB - 1, 0, -1))
                    P = {}
                    for _ in prep_gen(h, di, order[0], P):
                        pass
                    sgen = None
                    for i, bi in enumerate(order):
                        Pn = {}
                        pgen = prep_gen(h, di, order[i + 1], Pn) if i + 1 < len(order) else None
                        po, d_po = chunks(h, di, bi, P, pgen, sgen)
                        if pgen is not None:
                            for _ in pgen:
                                pass
                        if sgen is not None:
                            for _ in sgen:
                                pass
                        s1 = stage1_gen(h + 1, *Wn, blocks=(bi,)) if (di == 1 and Wn is not None) else None
                        sgen = seq_gen(finish_gen(h, di, bi, po, d_po), s1)
                        if di == 0:
                            for _ in sgen:
                                pass
                            sgen = None
                        P = Pn
                    if sgen is not None:
                        for _ in sgen:
                            pass

        if dbg == "C":
            kb.final_wait([d_of0] + list(dumps.values()))
            return nc, kb
        kb.barrier()

        d_x1 = Dep("x1", acc=True)
        with contextlib.ExitStack() as es5:
            Wo = sbuf("Wo0", [128, 16, D], BF16, es5)
            d_Wo = Dep()
            wos_r = sb_rot(nc, es5, "wos0_", 2, [128, 2, D], F32)
            for k2 in range(8):
                ws, d_ws = wos_r.next()
                kb.dma("sp", ws[:], hg_w_out[k2 * 256:(k2 + 1) * 256, :].rearrange("(k p) n -> p k n", p=128), d_ws, writes=[d_ws])
                kb.op("act", "activation", [d_ws], [d_Wo], out=Wo[:, k2 * 2:(k2 + 1) * 2, :], in_=ws[:], func=AF.Copy)
            gbc = sbuf("gbc0", [128, 2, D], F32, es5)
            d_gbc = Dep()
            kb.dma("sp", gbc[:], gate_dram[:, 0:2, :], d_gbc, reads=[d_gdram], writes=[d_gbc])
            ofb_r = sb_rot(nc, es5, "ofb0_", 2, [128, 16, 512], BF16)
            xt_r = sb_rot(nc, es5, "xt1_", 3, [128, D], F32)
            xnw_r = sb_rot(nc, es5, "xnw_", 3, [128, D], F32)
            psY = Rot([psum("psY%d" % i, [128, 512], F32, es5) for i in range(4)], excl=True)
            psT = psum("psT1", [128, 2, 4, 128], F32, es5)
            d_psT = [Dep(excl=True), Dep(excl=True)]
            cur = {}

            loaded = {}

            def load_ofb(bi):
                if bi in loaded or bi >= len(BLOCKS):
                    return
                t0, n = BLOCKS[bi]
                ofb, d_ofb = ofb_r.next()
                kb.dma("pool", ofb[:, :, 0:n], ofin0[:, :, t0:t0 + n].rearrange("h p t -> p h t"), d_ofb, reads=[d_of0], writes=[d_ofb])
                loaded[bi] = (ofb, d_ofb, t0)

            def get_x1(tt):
                for bi, (t0, n) in enumerate(BLOCKS):
                    if t0 <= tt * 128 < t0 + n:
                        break
                if cur.get("bi") != bi:
                    load_ofb(bi)
                    ofb, d_ofb, t0_ = loaded[bi]
                    cur.update(bi=bi, ofb=ofb, d_ofb=d_ofb, t0=t0_)
                    load_ofb(bi + 1)
                ofb, d_ofb = cur["ofb"], cur["d_ofb"]
                off = tt * 128 - cur["t0"]
                xt, d_x = xt_r.next()
                kb.dma("sp", xt[:], xin[tt * 128:(tt + 1) * 128, :], d_x, writes=[d_x])
                xnw, d_xnw = xnw_r.next()
                s_ = 1 if tt < 2 else 0
                for half in range(2):
                    py, d_py = psY.next()
                    hs = slice(half * 512, (half + 1) * 512)
                    for kc in range(16):
                        kb.op("pe", "matmul", [d_ofb, d_Wo], [d_py], inc=(kc == 15), out=py[:, :], lhsT=ofb[:, kc, off:off + 128],
                              rhs=Wo[:, kc, hs], start=(kc == 0), stop=(kc == 15))
                    kb.op("dve", "tensor_tensor", [d_py, d_gbc], [d_xnw], out=xnw[:, hs], in0=py[:, :], in1=gbc[:, s_, hs], op=ALU.mult)
                    kb.op("dve", "tensor_tensor", [d_xnw, d_x], [d_xnw], out=xnw[:, hs], in0=xnw[:, hs], in1=xt[:, hs], op=ALU.add)
                if tt >= 2:
                    kb.dma("pool", x1[tt * 128:(tt + 1) * 128, :], xnw[:], d_xnw, reads=[d_xnw], writes=[d_x1])
                return xnw[:], d_xnw
            prestage(1, es5, psT, d_psT, range(NT), get_x1)
        kb.barrier()

        if dbg == "D":
            dbg_out = nc.dram_tensor("dbg", [128, 8, T], BF16, kind="ExternalOutput").ap()
            d_dbg = Dep()
            kb.dma("sp", dbg_out[:, :, :], hT[:], d_dbg, reads=d_h, writes=[d_dbg])
            kb.final_wait([d_dbg, d_x1])
            return nc, kb

        d_of1 = Dep("ofin1", acc=True)
        with contextlib.ExitStack() as es6:
            ropeC_t = sbuf("ropeC_t", [128, L], BF16, es6)
            ropeS_t = sbuf("ropeS_t", [128, L], BF16, es6)
            d_rope = Dep()
            kb.dma("sp", ropeC_t[:], ropeC[:, :], d_rope, writes=[d_rope])
            kb.dma("sp", ropeS_t[:], ropeS[:, :], d_rope, writes=[d_rope])
            lv = sbuf("lv", [128, 4, 64], F32, es6)
            lp = sbuf("lp", [128, 2, 64], F32, es6)
            ls = sbuf("ls", [128, 4], F32, es6)
            sg = sbuf("sg", [128, 2], F32, es6)
            d_lv = Dep(); d_lp = Dep(); d_ls = Dep(); d_sg = Dep()
            kb.dma("sp", lv[:], lamv[:, :, :], d_lv, writes=[d_lv])
            kb.dma("sp", sg[:, 0:1], subg[:, :], d_sg, writes=[d_sg])
            kb.op("dve", "tensor_tensor", [d_lv], [d_lp], out=lp[:, 0, :], in0=lv[:, 0, :], in1=lv[:, 1, :], op=ALU.mult)
            kb.op("dve", "tensor_tensor", [d_lv, d_lp], [d_lp], out=lp[:, 1, :], in0=lv[:, 2, :], in1=lv[:, 3, :], op=ALU.mult)
            kb.op("dve", "tensor_reduce", [d_lp], [d_ls], out=ls[:, 0:2], in_=lp[:], axis=mybir.AxisListType.X, op=ALU.add)
            kb.op("act", "activation", [d_ls], [d_ls], out=ls[:, 0:2], in_=ls[:, 0:2], func=AF.Exp)
            kb.op("dve", "tensor_tensor", [d_ls], [d_ls], out=ls[:, 2:3], in0=ls[:, 1:2], in1=ls[:, 0:1], op=ALU.subtract)
            kb.op("dve", "tensor_scalar", [d_ls], [d_ls], out=ls[:, 3:4], in0=ls[:, 2:3], scalar1=-LAMBDA_INIT1, scalar2=None, op0=ALU.add)
            kb.op("dve", "tensor_scalar", [d_sg], [d_sg], out=sg[:, 1:2], in0=sg[:, 0:1], scalar1=(1.0 - LAMBDA_INIT1), scalar2=None, op0=ALU.mult)

            wst_r = sb_rot(nc, es6, "wsa", 1, [128, 8, 128], F32)
            NSLOT = 2
            Wb_s = [sbuf("Wba%d" % i, [128, 8, 4, 128], BF16, es6) for i in range(NSLOT)]
            qb_r = sb_rot(nc, es6, "qb16_", 2, [128, 512], BF16)
            perm_f = sbuf("perm_f", [128, 128], F32, es6)
            perm_b = sbuf("perm_b", [128, 128], BF16, es6)
            d_perm = Dep()
            kb.dma("sp", perm_f[:], permm[:, :], d_perm, writes=[d_perm])
            kb.op("dve", "tensor_copy", [d_perm], [d_perm], out=perm_b[:], in_=perm_f[:])
            qT_s = [sbuf("qTa%d" % i, [128, L], BF16, es6) for i in range(NSLOT)]
            kT_s = [sbuf("kTa%d" % i, [128, T], BF16, es6) for i in range(NSLOT)]
            Va_s = [sbuf("Vaa%d" % i, [128, NT, 130], BF16, es6) for i in range(NSLOT)]
            gs_s = [sbuf("gsa%d" % i, [128, L], BF16, es6) for i in range(NSLOT)]
            d_Wb = [Dep() for _ in range(NSLOT)]
            d_qT = [[Dep() for _ in range(8)] for _ in range(NSLOT)]
            d_kT = [[Dep() for _ in range(9)] for _ in range(NSLOT)]
            d_Va = [[Dep() for _ in range(9)] for _ in range(NSLOT)]
            d_gs = [[Dep() for _ in range(8)] for _ in range(NSLOT)]
            d_Vone = [Dep() for _ in range(NSLOT)]
            for i in range(NSLOT):
                kb.op("dve", "memset", [], [d_Vone[i]], ap=Va_s[i][:, :, 128:130], constant=1.0)
            tf_r = sb_rot(nc, es6, "tfa", 3, [128, 512], F32)
            PT_r = sb_rot(nc, es6, "PTa", 3, [128, 1024], BF16)
            accs_r = sb_rot(nc, es6, "accs", 1, [128, 8, 129], F32)
            sm_r = sb_rot(nc, es6, "sma", 4, [128, 16], F32)
            o_r = sb_rot(nc, es6, "oa", 4, [128, 128], F32)
            t_r = sb_rot(nc, es6, "ta", 2, [128, 128], F32)
            on_r = sb_rot(nc, es6, "ona", 4, [128, 128], F32)
            ofo_r = sb_rot(nc, es6, "ofoa", 2, [128, 512], BF16)
            psS = Rot([psum("psS%d" % i, [128, 1024], F32, es6) for i in range(2)], excl=True)
            accb = [psum("accb%d" % i, [128, 512], F32, es6) for i in range(3)]
            d_acc = Dep(excl=True)
            pm = psum("pmisc", [128, 512], F32, es6)
            d_pm = Dep(excl=True)

            pst = {"mid": False}

            def proj_gen(h, sl_):
                Wb, qT, kT, Va, gs = Wb_s[sl_], qT_s[sl_], kT_s[sl_], Va_s[sl_], gs_s[sl_]
                for wi in range(4):
                    wst, d_wst = wst_r.next()
                    c0 = wi * DI + h * 128
                    kb.dma("sp", wst[:], da_w_in[:, c0:c0 + 128].rearrange("(kc p) j -> p kc j", p=128), d_wst, writes=[d_wst])
                    kb.op("dve", "tensor_copy", [d_wst], [d_Wb[sl_]], out=Wb[:, :, wi, :], in_=wst[:])
                    yield
                dW = d_Wb[sl_]

                def half_group(slot, t0, c0, n):
                    hd = d_h[(t0 + c0) // 128:(t0 + c0 + n) // 128]
                    for kc in range(8):
                        kb.op("pe", "matmul", [dW] + hd, [d_pm], inc=(kc == 7), out=pm[:, c0:c0 + n], lhsT=Wb[:, kc, slot, :],
                              rhs=hT[:, kc, t0 + c0:t0 + c0 + n], start=(kc == 0), stop=(kc == 7))
                half_group(1, 0, 0, 256)
                kb.op("dve", "tensor_copy", [d_pm], [d_kT[sl_][0]], out=kT[:, 0:256], in_=pm[:, 0:256])
                yield
                for b in range(8):
                    t0 = 256 + 512 * b
                    l0 = 512 * b
                    for (slot, dst, ddst, doff) in ((0, qT, d_qT[sl_][b], l0), (1, kT, d_kT[sl_][b + 1], t0)):
                        half_group(slot, t0, 0, 256)
                        pst["mid"] = True
                        yield
                        half_group(slot, t0, 256, 256)
                        pst["mid"] = False
                        qb16, d_qb16 = qb_r.next()
                        kb.op("dve", "tensor_copy", [d_pm], [d_qb16], out=qb16[:], in_=pm[:, :])
                        t1, d_t1 = tf_r.next()
                        kb.op("dve", "tensor_tensor", [d_pm, d_rope], [d_t1], out=t1[:], in0=pm[:, :], in1=ropeC_t[:, l0:l0 + 512], op=ALU.mult)
                        yield
                        kb.op("pe", "matmul", [d_qb16, d_perm], [d_pm], out=pm[:, :], lhsT=perm_b[:], rhs=qb16[:], start=True, stop=True)
                        t2, d_t2 = tf_r.next()
                        kb.op("dve", "tensor_tensor", [d_pm, d_rope], [d_t2], out=t2[:], in0=pm[:, :], in1=ropeS_t[:, l0:l0 + 512], op=ALU.mult)
                        kb.op("dve", "tensor_tensor", [d_t1, d_t2], [ddst], out=dst[:, doff:doff + 512], in0=t1[:], in1=t2[:], op=ALU.add)
                        yield
                    half_group(3, t0, 0, 256)
                    pst["mid"] = True
                    yield
                    half_group(3, t0, 256, 256)
                    pst["mid"] = False
                    g_sb, d_gsb = tf_r.next()
                    kb.op("dve", "tensor_copy", [d_pm], [d_gsb], out=g_sb[:], in_=pm[:, :])
                    yield
                    yield
                    e_, d_e = tf_r.next()
                    kb.op("act", "activation", [d_gsb], [d_e], out=e_[:], in_=g_sb[:], func=AF.Exp, scale=-1.0)
                    kb.op("act", "activation", [d_e], [d_e], out=e_[:], in_=e_[:], func=AF.Ln, bias=1.0)
                    kb.op("act", "activation", [d_e], [d_e], out=e_[:], in_=e_[:], func=AF.Exp, scale=-1.0)
                    kb.op("dve", "tensor_tensor", [d_gsb, d_e], [d_gs[sl_][b]], out=gs[:, l0:l0 + 512], in0=g_sb[:], in1=e_[:], op=ALU.mult)
                    yield
                for gi in range(9):
                    tl0, ntl = (0, 2) if gi == 0 else (2 + 4 * (gi - 1), 4)
                    for ti in range(ntl):
                        tt = tl0 + ti
                        for kc in range(8):
                            kb.op("pe", "matmul", [dW, d_h[tt]], [d_pm], inc=(kc == 7), out=pm[:, ti * 128:(ti + 1) * 128],
                                  lhsT=hT[:, kc, tt * 128:(tt + 1) * 128], rhs=Wb[:, kc, 2, :], start=(kc == 0), stop=(kc == 7))
                        if ti < ntl - 1:
                            pst["mid"] = True
                            yield
                    pst["mid"] = False
                    kb.op("dve", "tensor_copy", [d_pm, d_Vone[sl_]], [d_Va[sl_][gi]], out=Va[:, tl0:tl0 + ntl, 0:128],
                          in_=pm[:, 0:ntl * 128].rearrange("p (a b) -> p a b", b=128))
                    yield

            def emit_S(h, qb, kbk):
                sl_ = h % NSLOT
                qT, kT = qT_s[sl_], kT_s[sl_]
                bk = 0 if kbk < 2 else 1 + (kbk - 2) // 4
                q0 = qb * 512
                ps, d_ps = psS.next()
                ksl = slice(kbk * 128, (kbk + 1) * 128)
                kb.op("pe", "matmul", [d_kT[sl_][bk], d_qT[sl_][qb]], [d_ps], inc=False, out=ps[:, 0:512], lhsT=kT[0:64, ksl],
                      rhs=qT[0:64, q0:q0 + 512], start=True, stop=True)
                kb.op("pe", "matmul", [d_kT[sl_][bk], d_qT[sl_][qb]], [d_ps], out=ps[:, 512:1024], lhsT=kT[64:128, ksl],
                      rhs=qT[64:128, q0:q0 + 512], start=True, stop=True)
                return ps, d_ps

            def emit_exp(ps, d_ps):
                PT, d_PT = PT_r.next()
                kb.op("act", "activation", [d_ps], [d_PT], out=PT[:], in_=ps[:, :], func=AF.Exp, scale=0.125)
                return PT, d_PT

            def emit_PV(h, qb, kbk, PT, d_PT):
                sl_ = h % NSLOT
                Va = Va_s[sl_]
                gk = 0 if kbk < 2 else 1 + (kbk - 2) // 4
                for half in range(2):
                    for qs in range(4):
                        i = half * 4 + qs
                        bank, slot = i // 3, i % 3
                        kb.op("pe", "matmul", [d_PT, d_Va[sl_][gk], d_Vone[sl_]], [d_acc], inc=(i == 7),
                              out=accb[bank][:, slot * 129:(slot + 1) * 129], lhsT=PT[:, half * 512 + qs * 128:half * 512 + (qs + 1) * 128],
                              rhs=Va[:, kbk, 0:129], start=(kbk == 0 and slot == 0), stop=(kbk == NT - 1), skip_group_check=True)

            def emit_readout(h, qb):
                sl_ = h % NSLOT
                gs = gs_s[sl_]
                q0 = qb * 512
                accs, d_accs = accs_r.next()
                af = accs[:].rearrange("p a b -> p (a b)")
                for bank in range(3):
                    cnt = 3 if bank < 2 else 2
                    kb.op("dve", "tensor_copy", [d_acc], [d_accs], out=af[:, bank * 387:bank * 387 + cnt * 129], in_=accb[bank][:, 0:cnt * 129])
                sm, d_sm = sm_r.next()
                kb.op("dve", "reciprocal", [d_accs], [d_sm], out=sm[:, 0:8], in_=accs[:, :, 128])
                kb.op("dve", "tensor_scalar", [d_sm, d_ls], [d_sm], out=sm[:, 8:12], in0=sm[:, 4:8], scalar1=ls[:, 3:4], scalar2=None, op0=ALU.mult)
                os_ = []
                for qs in range(4):
                    t_, d_t = t_r.next()
                    kb.op("dve", "tensor_scalar", [d_accs, d_sm], [d_t], out=t_[:], in0=accs[:, qs, 0:128], scalar1=sm[:, qs:qs + 1], scalar2=None,
                          op0=ALU.mult)
                    o_, d_o = o_r.next()
                    kb.op("dve", "scalar_tensor_tensor", [d_accs, d_sm, d_t], [d_o], out=o_[:], in0=accs[:, 4 + qs, 0:128],
                          scalar=sm[:, 8 + qs:9 + qs], in1=t_[:], op0=ALU.mult, op1=ALU.add)
                    kb.op("dve", "scalar_tensor_tensor", [d_o], [d_t, d_sm], out=t_[:], in0=o_[:], scalar=1.0, in1=o_[:], op0=ALU.mult, op1=ALU.mult,
                          accum_out=sm[:, 12 + qs:13 + qs])
                    os_.append((o_, d_o))
                ons = []

                def partB():
                    kb.op("act", "activation", [d_sm], [d_sm], out=sm[:, 12:16], in_=sm[:, 12:16], func=AF.Ln, scale=1.0 / 128, bias=EPS)
                    kb.op("act", "activation", [d_sm], [d_sm], out=sm[:, 12:16], in_=sm[:, 12:16], func=AF.Exp, scale=-0.5)
                    for qs in range(4):
                        o_, d_o = os_[qs]
                        on, d_on = on_r.next()
                        kb.op("dve", "tensor_scalar", [d_o, d_sm], [d_on], out=on[:], in0=o_[:], scalar1=sm[:, 12 + qs:13 + qs], scalar2=None,
                              op0=ALU.mult)
                        ons.append((on, d_on))

                def part2():
                    for qs in range(4):
                        on, d_on = ons[qs]
                        kb.op("pe", "transpose", [d_on, d_const], [d_pm], out=pm[:, qs * 128:(qs + 1) * 128], in_=on[:], identity=ident_t[:])
                    ofo, d_ofo = ofo_r.next()
                    kb.op("dve", "scalar_tensor_tensor", [d_pm, d_sg, d_gs[sl_][qb]], [d_ofo], out=ofo[:], in0=pm[:, :], scalar=sg[:, 1:2],
                          in1=gs[:, q0:q0 + 512], op0=ALU.mult, op1=ALU.mult)
                    kb.dma("pool", ofin1[h, :, q0:q0 + 512], ofo[:], d_ofo, reads=[d_ofo], writes=[d_of1])
                return partB, part2

            pg = proj_gen(0, 0)
            for _ in pg:
                pass
            its = [(h, qb, kbk) for h in range(heads1) for qb in range(8) for kbk in range(NT)]
            pg = None
            LOOK = 2
            NPSTEP = 104
            pg_done = [0]
            pend = []
            deferred = []
            started = set()

            def ensure_proj(hh):
                nonlocal pg
                if hh in started:
                    return
                if pg is not None:
                    for _ in pg:
                        pass
                    pg = None
                started.add(hh)

            started.add(0)
            for j in range(min(LOOK, len(its))):
                ensure_proj(its[j][0])
                pend.append(emit_S(*its[j]))
            for idx, (h, qb, kbk) in enumerate(its):
                if qb == 0 and kbk == 0:
                    if pg is not None:
                        for _ in pg:
                            pass
                    pg = proj_gen(h + 1, (h + 1) % NSLOT) if h + 1 < heads1 else None
                    pg_done[0] = 0
                PTd = emit_exp(*pend.pop(0))
                if idx + LOOK < len(its):
                    h2, qb2, kbk2 = its[idx + LOOK]
                    ensure_proj(h2)
                    pend.append(emit_S(h2, qb2, kbk2))
                emit_PV(h, qb, kbk, *PTd)
                if pg is not None:
                    it_h = qb * NT + kbk
                    need = min(NPSTEP, ((it_h + 1) * NPSTEP) // 250)
                    while pg_done[0] < need:
                        next(pg, None)
                        pg_done[0] += 1
                if kbk == NT - 1:
                    pB, p2 = emit_readout(h, qb)
                    deferred.append((idx + 9, pB))
                    deferred.append((idx + 15, p2))
                if deferred and (deferred[0][0] <= idx or idx == len(its) - 1):
                    while pst["mid"] and pg is not None:
                        if next(pg, "done") == "done":
                            break
                    while deferred and (deferred[0][0] <= idx or idx == len(its) - 1):
                        deferred.pop(0)[1]()
        kb.barrier()

        if dbg == "E":
            kb.final_wait([d_of1])
            return nc, kb

        d_out = Dep("out", acc=True)
        with contextlib.ExitStack() as es7:
            Wo = sbuf("Wo1", [128, 16, D], BF16, es7)
            d_Wo = Dep()
            wos_r = sb_rot(nc, es7, "wos1_", 2, [128, 2, D], F32)
            for k2 in range(8):
                ws, d_ws = wos_r.next()
                kb.dma("sp", ws[:], da_w_out[k2 * 256:(k2 + 1) * 256, :].rearrange("(k p) n -> p k n", p=128), d_ws, writes=[d_ws])
                kb.op("act", "activation", [d_ws], [d_Wo], out=Wo[:, k2 * 2:(k2 + 1) * 2, :], in_=ws[:], func=AF.Copy)
            gbc = sbuf("gbc1", [128, D], F32, es7)
            fg = sbuf("fg", [128, D], F32, es7)
            d_gbc = Dep(); d_fg = Dep()
            kb.dma("sp", gbc[:], gate_dram[:, 2, :], d_gbc, reads=[d_gdram], writes=[d_gbc])
            kb.dma("sp", fg[:], fgr[:, :], d_fg, writes=[d_fg])
            ofb_r = sb_rot(nc, es7, "ofb1_", 2, [128, 16, 512], BF16)
            xt_r = sb_rot(nc, es7, "xt2_", 3, [128, D], F32)
            xo_r = sb_rot(nc, es7, "xo_", 3, [128, D], F32)
            ot_r = sb_rot(nc, es7, "ot_", 2, [128, D], F32)
            junk = sbuf("junkF", [128, D], F32, es7)
            d_junk = Dep()
            st_r = sb_rot(nc, es7, "stF", 3, [128, 4], F32)
            psY = Rot([psum("psYF%d" % i, [128, 512], F32, es7) for i in range(4)], excl=True)
            pend_f = None
            for tt in range(L // 128):
                if tt % 4 == 0:
                    if tt == 0:
                        nxt_ofb = ofb_r.next()
                        kb.dma("pool", nxt_ofb[0][:], ofin1[:, :, 0:512].rearrange("h p t -> p h t"), nxt_ofb[1], reads=[d_of1], writes=[nxt_ofb[1]])
                    ofb, d_ofb = nxt_ofb
                    if tt + 4 < L // 128:
                        nxt_ofb = ofb_r.next()
                        kb.dma("pool", nxt_ofb[0][:], ofin1[:, :, (tt + 4) * 128:(tt + 4) * 128 + 512].rearrange("h p t -> p h t"), nxt_ofb[1],
                               reads=[d_of1], writes=[nxt_ofb[1]])
                off = (tt % 4) * 128
                xt, d_x = xt_r.next()
                kb.dma("sp", xt[:], x1[CTX + tt * 128:CTX + (tt + 1) * 128, :], d_x, reads=[d_x1], writes=[d_x])
                xo, d_xo = xo_r.next()
                for half in range(2):
                    py, d_py = psY.next()
                    hs = slice(half * 512, (half + 1) * 512)
                    for kc in range(16):
                        kb.op("pe", "matmul", [d_ofb, d_Wo], [d_py], inc=(kc == 15), out=py[:, :], lhsT=ofb[:, kc, off:off + 128],
                              rhs=Wo[:, kc, hs], start=(kc == 0), stop=(kc == 15))
                    kb.op("dve", "tensor_tensor", [d_py, d_gbc], [d_xo], out=xo[:, hs], in0=py[:, :], in1=gbc[:, hs], op=ALU.mult)
                    kb.op("dve", "tensor_tensor", [d_xo, d_x], [d_xo], out=xo[:, hs], in0=xo[:, hs], in1=xt[:, hs], op=ALU.add)
                st, d_st = st_r.next()
                kb.op("act", "activation", [d_xo], [d_junk, d_st], out=junk[:], in_=xo[:], func=AF.Square, accum_out=st[:, 0:1])
                kb.op("act", "activation", [d_st], [d_st], out=st[:, 1:2], in_=st[:, 0:1], func=AF.Ln, scale=1.0 / D, bias=EPS)
                kb.op("act", "activation", [d_st], [d_st], out=st[:, 2:3], in_=st[:, 1:2], func=AF.Exp, scale=-0.5)
                if pend_f is not None:
                    pend_f()

                def fin(tt=tt, xo=xo, d_xo=d_xo, st=st, d_st=d_st):
                    ot, d_ot = ot_r.next()
                    kb.op("dve", "scalar_tensor_tensor", [d_xo, d_st, d_fg], [d_ot], out=ot[:], in0=xo[:], scalar=st[:, 2:3], in1=fg[:],
                          op0=ALU.mult, op1=ALU.mult)
                    kb.dma("pool", out[tt * 128:(tt + 1) * 128, :], ot[:], d_ot, reads=[d_ot], writes=[d_out])
                pend_f = fin
            if pend_f is not None:
                pend_f()
        kb.final_wait([d_out])

    return nc, kb


def rope_tables():
    t = np.arange(L)
    row = (t // 64).astype(np.float64)
    col = (t % 64).astype(np.float64)
    inv = 1.0 / (10000.0 ** (np.arange(0, 32, 2, dtype=np.float64) / 32.0))
    f = np.arange(128)
    axis = (f % 64) // 32
    part = (f % 32) // 16
    i = f % 16
    pos = np.where(axis[:, None] == 0, row[None, :], col[None, :])
    ang = pos * inv[i][:, None]
    C = np.cos(ang)
    S = np.sin(ang) * np.where(part == 0, -1.0, 1.0)[:, None]
    import ml_dtypes
    return C.astype(np.float32).astype(ml_dtypes.bfloat16), S.astype(np.float32).astype(ml_dtypes.bfloat16)


def prep_inputs(b, x, c, ctx, c_ctx, w_ada, b_ada, norm_g, hg_w_in, hg_lb_logits, hg_norm_g, hg_w_out,
                da_w_in, da_lam_q1, da_lam_k1, da_lam_q2, da_lam_k2, da_subln_g, da_w_out, final_g, **kw):
    m = {}
    m["xin"] = np.ascontiguousarray(np.concatenate([ctx[b], x[b]], axis=0))
    cf = np.stack([c[b], c_ctx], axis=-1)
    m["cfm"] = np.ascontiguousarray(cf.reshape(8, 128, 2).transpose(1, 0, 2))
    m["w_ada"] = w_ada
    m["b_fm"] = np.ascontiguousarray(b_ada.reshape(2, 24, 128).transpose(0, 2, 1))
    m["b_row"] = np.ascontiguousarray(b_ada.reshape(2, 1, 3 * D))
    m["g_fm"] = np.ascontiguousarray(norm_g.reshape(2, 8, 128).transpose(0, 2, 1))
    m["ident"] = np.eye(128, dtype=np.float32)
    jj, tt = np.meshgrid(np.arange(128), np.arange(128), indexing="ij")
    m["maskf"] = (tt >= jj).astype(np.float32)
    m["maskb"] = (tt <= jj).astype(np.float32)
    m["hg_w_in"] = hg_w_in[0]
    m["lbl"] = np.ascontiguousarray(hg_lb_logits.reshape(2, 2, NH, 128).transpose(3, 0, 1, 2))
    m["hg_ng"] = np.ascontiguousarray(hg_norm_g[0].reshape(NH, 128).T)
    m["hg_w_out"] = hg_w_out[0]
    m["da_w_in"] = da_w_in[0]
    m["da_w_out"] = da_w_out[0]
    C, S = rope_tables()
    m["ropeC"] = C
    m["ropeS"] = S
    lv = np.stack([da_lam_q1[0], da_lam_k1[0], da_lam_q2[0], da_lam_k2[0]], axis=0)
    m["lamv"] = np.ascontiguousarray(np.broadcast_to(lv[None], (128, 4, 64))).astype(np.float32)
    m["subg"] = np.ascontiguousarray(da_subln_g[0].reshape(128, 1))
    pmat = np.zeros((128, 128), np.float32)
    pmat[np.arange(128) ^ 16, np.arange(128)] = 1.0
    m["permm"] = pmat
    m["fgr"] = np.ascontiguousarray(np.broadcast_to(final_g[None, :], (128, D))).astype(np.float32)
    return m


def kernel(**inputs):
    inputs = {k: np.asarray(v) for k, v in inputs.items()}
    nc, kb = build()
    in_maps = [prep_inputs(b, **inputs) for b in range(8)]
    res = run_bass_kernel_spmd(nc, in_maps, core_ids=list(range(8)))
    return np.stack([r["out"] for r in res.results], axis=0)
```
